# Optimizing a Trainium2 kernel written in Bass

```python
import jax
import jax.numpy as jnp
from jax import lax
import numpy as np


D_MODEL = 1024
BATCH = 2
SEQ = 8192
DEPTH = 4

CTX_LEN = 256
GRID_W = 64
D_CONV = 512
CONV_W = 31
DN_HEADS = 4
DN_HEAD_DIM = 128
D_DN = DN_HEADS * DN_HEAD_DIM
D_MIX = D_CONV + D_DN
SHORT_CONV_W = 5
CHUNK = 64
EPS = 1e-6
LN_EPS = 1e-5
QKV_START = 3 * D_CONV
SPLITS = (D_CONV, 2 * D_CONV, 3 * D_CONV, 3 * D_CONV + 3 * D_DN, 3 * D_CONV + 4 * D_DN,
          3 * D_CONV + 4 * D_DN + 2 * DN_HEADS)
REST_SPLITS = (3 * D_DN, 4 * D_DN, 4 * D_DN + 2 * DN_HEADS)
D_IN = 3 * D_CONV + 4 * D_DN + 4 * DN_HEADS

kernel_name = 'hymba_conformer_gated_deltanet_dit'


def rms_norm(x, w):
    xf = x.astype(jnp.float32)
    y = xf * lax.rsqrt(jnp.mean(xf * xf, axis=-1, keepdims=True) + EPS)
    return (y * w.astype(jnp.float32)).astype(x.dtype)


def layer_norm(x, w, b):
    xf = x.astype(jnp.float32)
    mu = jnp.mean(xf, axis=-1, keepdims=True)
    var = jnp.mean(jnp.square(xf - mu), axis=-1, keepdims=True)
    y = (xf - mu) * lax.rsqrt(var + LN_EPS)
    return (y * w.astype(jnp.float32) + b.astype(jnp.float32)).astype(x.dtype)


def l2norm(x):
    return x * lax.rsqrt(jnp.sum(x * x, axis=-1, keepdims=True) + EPS)


def dwconv1d(u, w):
    k, ch = w.shape
    pad = (k - 1) // 2
    return lax.conv_general_dilated(u, w.astype(u.dtype)[:, None, :], window_strides=(1,),
                                    padding=[(pad, pad)], dimension_numbers=('NWC', 'WIO', 'NWC'),
                                    feature_group_count=ch)


def seq_dwconv(u, w, b):
    return dwconv1d(u, w) + b.astype(u.dtype)


def grid_dwconv(u, w, b):
    bsz, t, ch = u.shape
    rows = t // GRID_W
    half = ch // 2
    g = u.reshape(bsz, rows, GRID_W, ch)
    uh = g[..., :half].reshape(bsz * rows, GRID_W, half)
    yh = dwconv1d(uh, w[:, :half]).reshape(bsz, rows, GRID_W, half)
    uv = g[..., half:].transpose(0, 2, 1, 3).reshape(bsz * GRID_W, rows, half)
    yv = dwconv1d(uv, w[:, half:]).reshape(bsz, GRID_W, rows, half).transpose(0, 2, 1, 3)
    return jnp.concatenate([yh, yv], axis=-1).reshape(bsz, t, ch) + b.astype(u.dtype)


def conv_branch(ga, gb, cgate, conv_w, conv_b, ln_w, ln_b, conv_fn):
    u = ga * jax.nn.sigmoid(gb)
    u = conv_fn(u, conv_w, conv_b)
    u = jax.nn.silu(layer_norm(u, ln_w, ln_b))
    return u * jax.nn.silu(cgate)


def chunk_gated_delta(q, k, v, g, beta, s0):
    bsz, h, t, dk = k.shape
    dv = v.shape[-1]
    n = t // CHUNK
    q = q * (dk ** -0.5)
    q, k, v = (m.reshape(bsz, h, n, CHUNK, m.shape[-1]) for m in (q, k, v))
    g = jnp.cumsum(g.reshape(bsz, h, n, CHUNK), axis=-1)
    beta = beta.reshape(bsz, h, n, CHUNK)[..., None]
    incl = jnp.tril(jnp.ones((CHUNK, CHUNK), dtype=bool))
    strict = jnp.tril(jnp.ones((CHUNK, CHUNK), dtype=bool), -1)
    diff = g[..., :, None] - g[..., None, :]
    decay = jnp.where(incl, jnp.exp(jnp.where(incl, diff, 0.0)), 0.0)
    kb = k * beta
    lmat = jnp.where(strict, jnp.einsum('bhncd,bhnsd->bhncs', kb, k) * decay, 0.0)
    eye = jnp.eye(CHUNK, dtype=jnp.float32)
    rhs = jnp.concatenate([v * beta, kb * jnp.exp(g)[..., None]], axis=-1)
    sol = lax.linalg.triangular_solve(eye + lmat, rhs, left_side=True, lower=True,
                                      unit_diagonal=True)
    u, w = sol[..., :dv], sol[..., dv:]
    attn = jnp.where(incl, jnp.einsum('bhncd,bhnsd->bhncs', q, k) * decay, 0.0)
    qg = q * jnp.exp(g)[..., None]
    kt = k * jnp.exp(g[..., -1:] - g)[..., None]
    glast = jnp.exp(g[..., -1])[..., None, None]

    def step(state, xs):
        attn_i, qg_i, kt_i, u_i, w_i, gl_i = xs
        v_new = u_i - jnp.einsum('bhcd,bhde->bhce', w_i, state)
        o_i = jnp.einsum('bhcd,bhde->bhce', qg_i, state) + jnp.einsum('bhcs,bhse->bhce', attn_i, v_new)
        state = state * gl_i + jnp.einsum('bhcd,bhce->bhde', kt_i, v_new)
        return state, o_i

    xs = tuple(jnp.moveaxis(m, 2, 0) for m in (attn, qg, kt, u, w, glast))
    state, o = lax.scan(step, s0, xs)
    o = jnp.moveaxis(o, 0, 2).reshape(bsz, h, t, dv)
    return o, state


def delta_branch(qkv, z, a, bt, sconv_w, a_log, dt_bias, norm_w, s0_f, s0_b):
    bsz, t, _ = qkv.shape
    qkv = jax.nn.silu(dwconv1d(qkv, sconv_w))
    heads = lambda m: m.reshape(bsz, t, DN_HEADS, DN_HEAD_DIM).transpose(0, 2, 1, 3).astype(jnp.float32)
    q, k, v = (heads(m) for m in jnp.split(qkv, 3, axis=-1))
    q, k = l2norm(q), l2norm(k)
    a = a.astype(jnp.float32).reshape(bsz, t, 2, DN_HEADS).transpose(2, 0, 3, 1)
    g = -jnp.exp(a_log.astype(jnp.float32))[:, None, :, None] * jax.nn.softplus(
        a + dt_bias.astype(jnp.float32)[:, None, :, None])
    beta = jax.nn.sigmoid(bt.astype(jnp.float32).reshape(bsz, t, 2, DN_HEADS).transpose(2, 0, 3, 1))
    o_f, s_f = chunk_gated_delta(q, k, v, g[0], beta[0], s0_f)
    flip = lambda m: jnp.flip(m, axis=2)
    o_b, s_b = chunk_gated_delta(flip(q), flip(k), flip(v), flip(g[1]), flip(beta[1]), s0_b)
    o = (o_f + flip(o_b)).transpose(0, 2, 1, 3)
    zg = jax.nn.silu(z.astype(jnp.float32).reshape(bsz, t, DN_HEADS, DN_HEAD_DIM))
    o = rms_norm(o, norm_w) * zg
    return o.reshape(bsz, t, D_DN).astype(z.dtype), s_f, s_b


def modulate(x, cond, norm_w, w_ada, b_ada):
    mod = jax.nn.silu(cond) @ w_ada + b_ada
    shift, scale, gate = (jnp.expand_dims(m, -2) for m in jnp.split(mod, 3, axis=-1))
    return rms_norm(x, norm_w) * (1 + scale) + shift, gate


def mixer_sublayer(x, cond, norm_w, w_ada, b_ada, w_in, conv_w, conv_b, cln_w, cln_b, sconv_w,
                   a_log, dt_bias, dn_norm_w, w_out, conv_fn, s0_f, s0_b):
    h, gate = modulate(x, cond, norm_w, w_ada, b_ada)
    proj = h @ w_in
    ga, gb, cgate, qkv, z, a, bt = jnp.split(proj, SPLITS, axis=-1)
    y_conv = conv_branch(ga, gb, cgate, conv_w, conv_b, cln_w, cln_b, conv_fn)
    y_dn, s_f, s_b = delta_branch(qkv, z, a, bt, sconv_w, a_log, dt_bias, dn_norm_w, s0_f, s0_b)
    out = jnp.concatenate([y_conv, y_dn], axis=-1) @ w_out
    return x + gate * out, s_f, s_b


def context_states(x, cond, norm_w, w_ada, b_ada, w_in, sconv_w, a_log, dt_bias, dn_norm_w, s0_f, s0_b):
    h, _ = modulate(x, cond, norm_w, w_ada, b_ada)
    qkv, z, a, bt = jnp.split(h @ w_in[:, QKV_START:], REST_SPLITS, axis=-1)
    _, s_f, s_b = delta_branch(qkv, z, a, bt, sconv_w, a_log, dt_bias, dn_norm_w, s0_f, s0_b)
    return s_f, s_b


def setup_inputs(seed: int = 0) -> dict:
    key = jax.random.key(seed)
    ks = jax.random.split(key, 20)
    f32 = jnp.float32
    nrm = lambda k, shape, s: jax.random.normal(k, shape, f32) * s
    a_init = jnp.log(jax.random.uniform(ks[14], (DEPTH, 2, DN_HEADS), f32, 1.0, 16.0))
    dt = jnp.exp(jax.random.uniform(ks[15], (DEPTH, 2, DN_HEADS), f32, math_log(0.001), math_log(0.1)))
    dt_bias = dt + jnp.log(-jnp.expm1(-dt))
    return {
        'x': nrm(ks[0], (BATCH, SEQ, D_MODEL), 1.0),
        'c': nrm(ks[1], (BATCH, D_MODEL), 1.0),
        'ctx': nrm(ks[2], (BATCH, CTX_LEN, D_MODEL), 1.0),
        'c_ctx': nrm(ks[3], (D_MODEL,), 1.0),
        'norm_w': 1.0 + nrm(ks[4], (DEPTH, D_MODEL), 0.1),
        'w_ada': nrm(ks[5], (DEPTH, D_MODEL, 3 * D_MODEL), 0.5 * D_MODEL ** -0.5),
        'b_ada': nrm(ks[6], (DEPTH, 3 * D_MODEL), 0.02),
        'w_in': nrm(ks[7], (DEPTH, D_MODEL, D_IN), D_MODEL ** -0.5),
        'conv_w': nrm(ks[8], (DEPTH, CONV_W, D_CONV), CONV_W ** -0.5),
        'conv_b': nrm(ks[9], (DEPTH, D_CONV), 0.02),
        'conv_ln_w': 1.0 + nrm(ks[10], (DEPTH, D_CONV), 0.1),
        'conv_ln_b': nrm(ks[11], (DEPTH, D_CONV), 0.02),
        'short_conv_w': nrm(ks[12], (DEPTH, SHORT_CONV_W, 3 * D_DN), SHORT_CONV_W ** -0.5),
        'a_log': a_init,
        'dt_bias': dt_bias,
        'dn_norm_w': 1.0 + nrm(ks[16], (DEPTH, DN_HEAD_DIM), 0.1),
        'w_out': nrm(ks[17], (DEPTH, D_MIX, D_MODEL), D_MIX ** -0.5),
        'final_norm_w': 1.0 + nrm(ks[18], (D_MODEL,), 0.1),
    }


def math_log(v):
    return float(np.log(v))


def reference(x, c, ctx, c_ctx, norm_w, w_ada, b_ada, w_in, conv_w, conv_b, conv_ln_w, conv_ln_b,
              short_conv_w, a_log, dt_bias, dn_norm_w, w_out, final_norm_w):
    bsz = x.shape[0]
    zeros = jnp.zeros((bsz, DN_HEADS, DN_HEAD_DIM, DN_HEAD_DIM), jnp.float32)
    for l in range(DEPTH):
        if l < DEPTH - 1:
            ctx_new, s_f, s_b = mixer_sublayer(
                ctx, c_ctx, norm_w[l], w_ada[l], b_ada[l], w_in[l], conv_w[l], conv_b[l],
                conv_ln_w[l], conv_ln_b[l], short_conv_w[l], a_log[l], dt_bias[l], dn_norm_w[l],
                w_out[l], seq_dwconv, zeros, zeros)
        else:
            ctx_new = ctx
            s_f, s_b = context_states(
                ctx, c_ctx, norm_w[l], w_ada[l], b_ada[l], w_in[l], short_conv_w[l], a_log[l],
                dt_bias[l], dn_norm_w[l], zeros, zeros)
        x, _, _ = mixer_sublayer(
            x, c, norm_w[l], w_ada[l], b_ada[l], w_in[l], conv_w[l], conv_b[l], conv_ln_w[l],
            conv_ln_b[l], short_conv_w[l], a_log[l], dt_bias[l], dn_norm_w[l], w_out[l],
            grid_dwconv, s_f, s_b)
        ctx = ctx_new
    return rms_norm(x, final_norm_w)
```

```python
import os
import numpy as np
from contextlib import ExitStack
import concourse.bass as bass
import concourse.mybir as mybir
from concourse.bass_utils import run_bass_kernel_spmd

F32 = mybir.dt.float32
BF16 = mybir.dt.bfloat16
AF = mybir.ActivationFunctionType
OP = mybir.AluOpType
AX = mybir.AxisListType

P = 128
DM = 1024
KC = 8
DIN = 3600
TC = 256
NB = 256
BIG = 1.0e4
ENGS = ("pe", "act", "dve", "pool", "sp")
STOP = int(os.environ.get("KSTOP", "9"))


class Buf:
    __slots__ = ("name", "last_w", "readers", "excl")

    def __init__(self, name, excl=False):
        self.name, self.last_w, self.readers, self.excl = name, None, [], excl


class Node:
    __slots__ = ("eng", "fn", "deps", "dma", "key", "signal", "cnt", "ep")

    def __init__(self, eng, fn, deps, dma, key):
        self.eng, self.fn, self.deps, self.dma, self.key = eng, fn, deps, dma, key
        self.signal, self.cnt, self.ep = dma, 0, 0


class Asm:
    def __init__(self):
        self.ins = {e: [] for e in ENGS}
        self.dma_keys = {}
        self.final_waits = []

    def op(self, eng, fn, reads=(), writes=(), dma=False, key=None):
        ex = [b for b in reads if b.excl]
        if ex:
            reads = [b for b in reads if not b.excl]
            writes = list(writes) + [b for b in ex if b not in writes]
        deps = []
        for b in reads:
            if b.last_w is not None:
                deps.append(b.last_w)
        for b in writes:
            if b.last_w is not None:
                deps.append(b.last_w)
            deps.extend(b.readers)
        if dma:
            if key is None:
                key = writes[0].name
            self.dma_keys.setdefault(key, 0)
        node = Node(eng, fn, deps, dma, key)
        for b in reads:
            b.readers.append(node)
        for b in writes:
            b.last_w = node
            b.readers = []
        self.ins[eng].append(node)
        return node

    def emit(self, nc):
        for e in ENGS:
            for n in self.ins[e]:
                for d in n.deps:
                    if d is n or d.dma:
                        continue
                    if d.eng == "pe" and n.eng == "pe" and not n.dma:
                        continue
                    d.signal = True
        keycnt = {k: 0 for k in self.dma_keys}
        EPOCH = int(os.environ.get("KEPOCH", "16000"))
        nep = {e: 1 for e in ENGS}
        for e in ENGS:
            c = 0
            ep = 0
            for n in self.ins[e]:
                if n.dma:
                    keycnt[n.key] += 16
                    n.cnt = keycnt[n.key]
                elif n.signal:
                    if c >= EPOCH:
                        c = 0
                        ep += 1
                    c += 1
                    n.cnt, n.ep = c, ep
            nep[e] = ep + 1
        assert max(keycnt.values() or [0]) < 32000
        with ExitStack() as st:
            esem = {(e, i): st.enter_context(nc.semaphore("sem_%s%d" % (e, i))) for e in ENGS if e != "sp" for i in range(nep[e])}
            ksem = {k: st.enter_context(nc.semaphore("dk_%d" % i)) for i, k in enumerate(self.dma_keys)}
            block = st.enter_context(nc.Block())
            lastk = {}
            for e in ENGS:
                for n in self.ins[e]:
                    if n.dma and n.key in self.final_waits:
                        lastk[n.key] = max(lastk.get(n.key, 0), n.cnt)

            def run(e, eng):
                waited = {}
                for n in self.ins[e]:
                    for d in n.deps:
                        if d is n:
                            continue
                        if d.dma:
                            s, v, kk = ksem[d.key], d.cnt, ("k", d.key)
                        else:
                            if d.eng == "pe" and e == "pe" and not n.dma:
                                continue
                            s, v, kk = esem[(d.eng, d.ep)], d.cnt, ("e", d.eng, d.ep)
                        if waited.get(kk, 0) >= v:
                            continue
                        waited[kk] = v
                        eng.wait_ge(s, v)
                    ins = n.fn(eng)
                    if n.dma:
                        ins.then_inc(ksem[n.key], 16)
                    elif n.signal:
                        ins.then_inc(esem[(e, n.ep)], 1)
                if e == "sp":
                    for k, v in lastk.items():
                        eng.wait_ge(ksem[k], v)

            block.tensor(lambda eng: run("pe", eng))
            block.scalar(lambda eng: run("act", eng))
            block.vector(lambda eng: run("dve", eng))
            block.gpsimd(lambda eng: run("pool", eng))
            block.sync(lambda eng: run("sp", eng))


def vec_layout(L):
    off = {}
    o = 0
    per = [("normw", 8), ("bada", 24), ("convw", 124), ("convb", 4), ("lnw", 4), ("lnb", 4),
           ("scw", 60), ("alog", 8), ("dtb", 8), ("dnw", 128)]
    for l in range(L):
        for nm, n in per:
            off[(nm, l)] = o
            o += n
    off["fnw"] = o
    o += 8
    off["cond"] = o
    o += 16
    return off, o


def build_program(R, L):
    TL = 64 * R
    T = TC + TL
    NT = T // P
    assert TL % NB == 0
    blocks = [(0, NB, 0)] + [(TC + i * NB, NB, 1) for i in range(TL // NB)]
    voff, NV = vec_layout(L)

    nc = bass.Bass("TRN2", target_bir_lowering=False)
    d_in = lambda n, s: nc.dram_tensor(n, s, F32, kind="ExternalInput").ap()
    xT = d_in("xT", [DM, T])
    vec_d = d_in("vec_in", [P, NV])
    cst_d = d_in("cst_in", [P, 8 * P])
    w_ada = d_in("w_ada", [L, DM, 3 * DM])
    w_in = d_in("w_in", [L, DM, DIN])
    w_out = d_in("w_out", [L, DM, DM])
    outT = nc.dram_tensor("outT", [DM, TL], F32, kind="ExternalOutput").ap()
    XS = nc.dram_tensor("XS", [DM, T], F32).ap()
    Ud = nc.dram_tensor("Ud", [512, T], F32).ap()
    CGd = nc.dram_tensor("CGd", [512, T], F32).ap()
    QKVd = nc.dram_tensor("QKVd", [1536, T], F32).ap()
    ZGd = nc.dram_tensor("ZGd", [512, T], F32).ap()
    CVd = nc.dram_tensor("CVd", [512, T], F32).ap()
    YDd = nc.dram_tensor("YDd", [512, T], BF16).ap()

    A = Asm()
    bufs = {}

    def B(name):
        if name not in bufs:
            bufs[name] = Buf(name)
        return bufs[name]

    with ExitStack() as st:
        sb = lambda n, s, d=F32: st.enter_context(nc.sbuf_tensor(n, s, d))
        pst = lambda n, s: st.enter_context(nc.psum_tensor(n, s, F32))
        winb = sb("winb", [P, KC * DIN], BF16)
        BIGA = sb("BIGA", [P, T])
        BIGB = sb("BIGB", [P, T])
        wst = sb("wst", [P, 1024])
        xb = sb("xb", [P, KC * NB])
        hb = sb("hb", [P, KC * NB], BF16)
        sqb = [sb("sqb%d" % i, [P, NB], BF16) for i in range(2)]
        ev = [sb("ev%d" % i, [P, NB]) for i in range(4)]
        cvb = sb("cvb", [P, 4 * NB])
        cgb = sb("cgb", [P, 4 * NB])
        rstd = sb("rstd", [P, NB])
        mean = sb("mean", [P, NB])
        vec = sb("vec", [P, NV])
        cst = sb("cst", [P, 8 * P])
        cstb = sb("cstb", [P, 2 * P], BF16)
        epsc = sb("epsc", [P, 4])
        sc = sb("sc", [P, 16])
        modt = sb("modt", [P, 48])
        gwt = sb("gwt", [P, 16])
        ABT = sb("ABT", [P, NT * 16])
        SC = {n: sb("S_" + n, [P, 8 * NT]) for n in ("G", "GC", "GT", "EG", "GL", "BE", "NBE")}
        SC["EKT"] = SC["GT"]
        SC["BKG"] = SC["G"]
        eal = sb("eal", [P, 8])
        tmpc = sb("tmpc", [P, NT])
        tmpc2 = sb("tmpc2", [P, NT])
        orstd = sb("orstd", [P, NT])
        DTF = ("Gd", "D", "DMs", "N0", "U0", "N1", "U1", "Y", "u", "t")
        DTB = ("attn", "attnT", "TTb", "wT", "kbg", "kt", "vb", "vn")
        DT = {}
        for sl in range(2):
            for n in DTF:
                DT[(n, sl)] = sb("d_%s%d" % (n, sl), [P, P])[:, :]
            for n in DTB:
                DT[(n, sl)] = sb("d_%s%d" % (n, sl), [P, P], BF16)[:, :]
        off = 0
        NSTEP = 3
        need = (2 * NSTEP - 2) * (len(DTF) * P + len(DTB) * P // 2)
        carve = BIGA if need <= T else sb("dtx", [P, need])
        for sl in range(2, 2 * NSTEP):
            for n in DTF:
                DT[(n, sl)] = carve[:, off:off + P]
                off += P
            for n in DTB:
                DT[(n, sl)] = carve[:, off:off + P // 2].bitcast(BF16)
                off += P // 2
        ST = {}
        for dr in range(2):
            ST[("Sf", dr)] = sb("d_Sf%d" % dr, [P, P])[:, :]
            ST[("Sb", dr)] = sb("d_Sb%d" % dr, [P, P], BF16)[:, :]
        fz = sb("fz", [P, 1])
        ont = sb("ont", [P, P])
        ydt = sb("ydt", [P, P], BF16)

        bank = [pst("bank%d" % i, [P, 512]) for i in range(8)]
        for i in range(8):
            bufs["bank%d" % i] = Buf("bank%d" % i, excl=True)
        for nm, i in (("pA0", 0), ("pA1", 1), ("pS", 2), ("pM", 3)):
            bufs[nm] = bufs["bank%d" % i]
        pA = [bank[0], bank[1]]
        pS, pM = bank[2], bank[3]
        PHYS = {0: 0, 1: 1, 2: 2, 3: 3, 4: 2, 5: 0}

        def pd(i):
            ch_, lg_ = i // 6, i % 6
            c0 = PHYS[lg_] * P
            return bank[ch_][:, c0:c0 + P], bufs["bank%d" % ch_]

        def psum_fence():
            pass

        ident = cst[:, 0:P]
        ones = cst[:, P:2 * P]
        tri = [cst[:, 2 * P:3 * P], cst[:, 3 * P:4 * P]]
        pen = [cst[:, 4 * P:5 * P], cst[:, 5 * P:6 * P]]
        msk = [cst[:, 6 * P:7 * P], cst[:, 7 * P:8 * P]]
        identb = cstb[:, 0:P]
        onesb = cstb[:, P:2 * P]
        V = lambda key, i=0, n=1: vec[:, voff[key] + i: voff[key] + i + n]

        woutb = BIGA[:, :].bitcast(BF16) if 2 * T >= KC * DM else sb("woutb", [P, KC * DM], BF16)[:, :]
        qT = winb[:, 0:T]
        kT = winb[:, T:2 * T]
        VTt = winb[:, 2 * T:3 * T]
        Bwin, BA, BB = B("winb"), B("BIGA"), B("BIGB")

        def dma(out, in_, reads, writes, eng="sp", key=None):
            A.op(eng, lambda e: e.dma_start(out=out, in_=in_), reads=reads, writes=writes, dma=True, key=key)

        def act(out, in_, func, reads, writes, bias=None, scale=None):
            kw = {}
            if bias is not None:
                kw["bias"] = bias
            if scale is not None:
                kw["scale"] = scale
            A.op("act", lambda e: e.activation(out=out, in_=in_, func=func, **kw), reads=reads, writes=writes)

        def mm(out, lhsT, rhs, reads, writes, start=True, stop=True):
            A.op("pe", lambda e: e.matmul(out, lhsT=lhsT, rhs=rhs, start=start, stop=stop), reads=reads, writes=writes)

        def tt(out, in0, in1, op, reads, writes, eng="dve"):
            A.op(eng, lambda e: e.tensor_tensor(out=out, in0=in0, in1=in1, op=op), reads=reads, writes=writes)

        def ts(out, in0, s1, s2, op0, op1, reads, writes, eng="dve"):
            if s2 is None:
                A.op(eng, lambda e: e.tensor_scalar(out=out, in0=in0, scalar1=s1, scalar2=None, op0=op0), reads=reads, writes=writes)
            else:
                A.op(eng, lambda e: e.tensor_scalar(out=out, in0=in0, scalar1=s1, scalar2=s2, op0=op0, op1=op1), reads=reads, writes=writes)

        def stt(out, in0, s, in1, op0, op1, reads, writes, eng="dve"):
            A.op(eng, lambda e: e.scalar_tensor_tensor(out=out, in0=in0, scalar=s, in1=in1, op0=op0, op1=op1), reads=reads, writes=writes)

        def cp(out, in_, reads, writes, eng="dve"):
            if eng == "act":
                act(out, in_, AF.Copy, reads, writes)
            else:
                A.op(eng, lambda e: e.tensor_copy(out=out, in_=in_), reads=reads, writes=writes)

        def recip(out, in_, reads, writes):
            A.op("dve", lambda e: e.reciprocal(out=out, in_=in_), reads=reads, writes=writes)

        Bv, Bc = B("vec"), B("cst")
        dma(vec[:, :], vec_d[:, :], [], [Bv])
        dma(cst[:, :], cst_d[:, :], [], [Bc])
        cp(cstb[:, :], cst[:, 0:2 * P], [Bc], [B("cstb")])
        Bcb = B("cstb")
        Beps = B("epsc")
        A.op("pool", lambda e: e.memset(epsc[:, 0:1], 1e-6), writes=[Beps])
        A.op("pool", lambda e: e.memset(epsc[:, 1:2], 1e-5), writes=[Beps])
        A.op("pool", lambda e: e.memset(epsc[:, 2:3], 1.0), writes=[Beps])
        A.op("pool", lambda e: e.memset(epsc[:, 3:4], 0.0), writes=[Beps])
        EPS6, EPS5, ONE1 = epsc[:, 0:1], epsc[:, 1:2], epsc[:, 2:3]
        act(sc[:, :], V("cond", 0, 16), AF.Silu, [Bv], [B("sc")])

        rr = {"ev": 0, "pA": 0, "sq": 0}

        def nxt(k, n):
            i = rr[k]
            rr[k] = (i + 1) % n
            return i

        def load_weight_bf16(dst, dstB, src_rows, ncols, coloff=0):
            c0 = 0
            i = 0
            while c0 < ncols:
                w = min(1024, ncols - c0)
                dma(wst[:, 0:w], src_rows[:, c0:c0 + w], [], [B("wst")])
                eng = ("act", "dve", "pool")[i % 3]
                cp(dst[:, coloff + c0: coloff + c0 + w], wst[:, 0:w], [B("wst")], [dstB], eng=eng)
                c0 += w
                i += 1

        def rms_stats(x3, xB, nb, out_rstd, inv_n, epsap):
            for k in range(KC):
                si = nxt("sq", 2)
                act(sqb[si][:, 0:nb], x3(k), AF.Square, [xB], [B("sqb%d" % si)])
                mm(pS[:, 0:nb], onesb, sqb[si][:, 0:nb], [Bcb, B("sqb%d" % si)], [B("pS")], start=(k == 0), stop=(k == KC - 1))
            act(out_rstd[:, 0:nb], pS[:, 0:nb], AF.Sqrt, [B("pS"), Beps], [B("rstd")], bias=epsap, scale=inv_n)
            recip(out_rstd[:, 0:nb], out_rstd[:, 0:nb], [B("rstd")], [B("rstd")])

        for l in range(L):
            last = (l == L - 1)
            xsrc = xT if l == 0 else XS
            xsrcB = B("xT") if l == 0 else B("XS")
            Bmod = B("modt")
            for k in range(KC):
                for piece in range(3):
                    dma(wst[:, :], w_ada[l, k * P:(k + 1) * P, piece * 1024:(piece + 1) * 1024], [], [B("wst")])
                    for jj in range(8):
                        j = piece * 8 + jj
                        mm(pM[:, j * 2:j * 2 + 2], wst[:, jj * P:(jj + 1) * P], sc[:, k * 2:k * 2 + 2], [B("wst"), B("sc")], [B("pM")])
                if k == 0:
                    cp(modt[:, :], pM[:, 0:48], [B("pM")], [Bmod])
                else:
                    tt(modt[:, :], modt[:, :], pM[:, 0:48], OP.add, [B("pM"), Bmod], [Bmod])
            for j in range(24):
                ts(modt[:, 2 * j:2 * j + 2], modt[:, 2 * j:2 * j + 2], V(("bada", l), j), None, OP.add, None, [Bmod, Bv], [Bmod])
            for k in range(KC):
                ts(gwt[:, 2 * k:2 * k + 2], modt[:, 2 * (8 + k):2 * (8 + k) + 2], 1.0, None, OP.add, None, [Bmod], [B("gwt")])
                ts(gwt[:, 2 * k:2 * k + 2], gwt[:, 2 * k:2 * k + 2], V(("normw", l), k), None, OP.mult, None, [B("gwt"), Bv], [B("gwt")])
            SH = lambda k, c: modt[:, 2 * k + c:2 * k + c + 1]
            GW = lambda k, c: gwt[:, 2 * k + c:2 * k + c + 1]
            GATE = lambda k, c: modt[:, 2 * (16 + k) + c:2 * (16 + k) + c + 1]

            if STOP < 1:
                break
            for k in range(KC):
                load_weight_bf16(winb, Bwin, w_in[l, k * P:(k + 1) * P, :], DIN, coloff=k * DIN)

            xv = xb[:, :].rearrange("p (k n) -> p k n", k=KC)
            hv = hb[:, :].rearrange("p (k n) -> p k n", k=KC)
            Bx, Bh = B("xb"), B("hb")
            for (t0, nb, c) in blocks:
                dma(xv[:, :, 0:nb], xsrc.rearrange("(k p) t -> p k t", p=P)[:, :, t0:t0 + nb], [xsrcB], [Bx])
                rms_stats(lambda k: xv[:, k, 0:nb], Bx, nb, rstd, 1.0 / DM, EPS6)
                for k in range(KC):
                    i = nxt("ev", 4)
                    tt(ev[i][:, 0:nb], xv[:, k, 0:nb], rstd[:, 0:nb], OP.mult, [Bx, B("rstd")], [B("ev%d" % i)])
                    act(hv[:, k, 0:nb], ev[i][:, 0:nb], AF.Identity, [B("ev%d" % i), Bmod, B("gwt")], [Bh], bias=SH(k, c), scale=GW(k, c))

                def proj(ct, w=P):
                    pi = nxt("pA", 2)
                    for k in range(KC):
                        mm(pA[pi][0:w, 0:nb], winb[:, k * DIN + ct * P: k * DIN + ct * P + w], hv[:, k, 0:nb], [Bwin, Bh], [B("pA%d" % pi)], start=(k == 0), stop=(k == KC - 1))
                    return pA[pi], B("pA%d" % pi)

                skip_conv = last and c == 0
                if not skip_conv:
                    for ct in range(4):
                        pg, pgB = proj(ct + 4)
                        i = nxt("ev", 4)
                        act(ev[i][:, 0:nb], pg[:, 0:nb], AF.Sigmoid, [pgB], [B("ev%d" % i)])
                        pa_, paB = proj(ct)
                        j = nxt("ev", 4)
                        tt(ev[j][:, 0:nb], pa_[:, 0:nb], ev[i][:, 0:nb], OP.mult, [paB, B("ev%d" % i)], [B("ev%d" % j)])
                        dma(Ud[ct * P:(ct + 1) * P, t0:t0 + nb], ev[j][:, 0:nb], [B("ev%d" % j)], [B("Ud")], eng="pool", key="st_ev%d" % j)
                    for ct in range(4):
                        pg, pgB = proj(8 + ct)
                        i = nxt("ev", 4)
                        act(ev[i][:, 0:nb], pg[:, 0:nb], AF.Silu, [pgB], [B("ev%d" % i)])
                        dma(CGd[ct * P:(ct + 1) * P, t0:t0 + nb], ev[i][:, 0:nb], [B("ev%d" % i)], [B("CGd")], eng="pool", key="st_ev%d" % i)
                for ct in range(12):
                    pg, pgB = proj(12 + ct)
                    i = nxt("ev", 4)
                    cp(ev[i][:, 0:nb], pg[:, 0:nb], [pgB], [B("ev%d" % i)], eng=("dve" if ct % 2 else "act"))
                    dma(QKVd[ct * P:(ct + 1) * P, t0:t0 + nb], ev[i][:, 0:nb], [B("ev%d" % i)], [B("QKVd")], eng="pool", key="st_ev%d" % i)
                for ct in range(4):
                    pg, pgB = proj(24 + ct)
                    i = nxt("ev", 4)
                    act(ev[i][:, 0:nb], pg[:, 0:nb], AF.Silu, [pgB], [B("ev%d" % i)])
                    dma(ZGd[ct * P:(ct + 1) * P, t0:t0 + nb], ev[i][:, 0:nb], [B("ev%d" % i)], [B("ZGd")], eng="pool", key="st_ev%d" % i)
                for tt_i in range(nb // P):
                    ti = t0 // P + tt_i
                    for k in range(KC):
                        mm(pM[:, 64:80], hv[:, k, tt_i * P:(tt_i + 1) * P], winb[:, k * DIN + 3584:k * DIN + 3600], [Bh, Bwin], [B("pM")], start=(k == 0), stop=(k == KC - 1))
                    cp(ABT[:, ti * 16:(ti + 1) * 16], pM[:, 64:80], [B("pM")], [B("ABT")])

            if STOP < 2:
                break
            abv = ABT[:, :].rearrange("p (t r) -> p t r", r=16)
            S3 = {n: SC[n][:, :].rearrange("p (r t) -> p r t", r=8) for n in SC}
            BS = B("SC")
            act(eal[:, :], V(("alog", l), 0, 8), AF.Exp, [Bv], [B("eal")])
            ts(eal[:, :], eal[:, :], -1.0, None, OP.mult, None, [B("eal")], [B("eal")])
            for r in range(8):
                ts(tmpc[:, :], abv[:, :, r], V(("dtb", l), r), None, OP.add, None, [B("ABT"), Bv], [B("tmpc")])
                act(tmpc2[:, :], tmpc[:, :], AF.Abs, [B("tmpc")], [B("tmpc2")])
                act(tmpc2[:, :], tmpc2[:, :], AF.Exp, [B("tmpc2")], [B("tmpc2")], scale=-1.0)
                act(tmpc2[:, :], tmpc2[:, :], AF.Ln, [B("tmpc2"), Beps], [B("tmpc2")], bias=ONE1)
                stt(tmpc[:, :], tmpc[:, :], 0.0, tmpc2[:, :], OP.max, OP.add, [B("tmpc"), B("tmpc2")], [B("tmpc")])
                ts(S3["G"][:, r, :], tmpc[:, :], eal[:, r:r + 1], None, OP.mult, None, [B("tmpc"), B("eal")], [BS])
                act(S3["BE"][:, r, :], abv[:, :, 8 + r], AF.Sigmoid, [B("ABT")], [BS])
            for dr in range(2):
                sl = slice(dr * 4 * NT, (dr + 1) * 4 * NT)
                mm(pS[:, 0:4 * NT], tri[dr], SC["G"][:, sl], [Bc, BS], [B("pS")])
                cp(SC["GC"][:, sl], pS[:, 0:4 * NT], [B("pS")], [BS])
                mm(pS[:, 0:4 * NT], ones, SC["G"][:, sl], [Bc, BS], [B("pS")])
                cp(SC["GT"][:, sl], pS[:, 0:4 * NT], [B("pS")], [BS])
            act(SC["EG"][:, :], SC["GC"][:, :], AF.Exp, [BS], [BS])
            act(SC["GL"][:, :], SC["GT"][:, :], AF.Exp, [BS], [BS])
            tt(SC["EKT"][:, :], SC["GT"][:, :], SC["GC"][:, :], OP.subtract, [BS], [BS])
            act(SC["EKT"][:, :], SC["EKT"][:, :], AF.Exp, [BS], [BS])
            ts(SC["NBE"][:, :], SC["BE"][:, :], -1.0, None, OP.mult, None, [BS], [BS])
            tt(SC["BKG"][:, :], SC["BE"][:, :], SC["EG"][:, :], OP.mult, [BS], [BS])
            SCL = lambda n, dr, hd, ti: SC[n][:, (dr * 4 + hd) * NT + ti:(dr * 4 + hd) * NT + ti + 1]

            if STOP < 3:
                break
            def dwconv(src, srcBs, dst, dstBs, wcol, ntap, first_bias, mode, do_ctx=True):
                half = (ntap - 1) // 2
                segs = []
                if do_ctx:
                    segs.append(("ctx", "dve"))
                segs.append(("lat", "dve"))
                for seg, eng in segs:
                    sB, dB = srcBs[seg], dstBs[seg]
                    lo, hi = (0, TC) if seg == "ctx" else (TC, T)
                    if first_bias is None:
                        ts(dst[:, lo:hi], src[:, lo:hi], wcol(half), None, OP.mult, None, [sB, Bv], [dB], eng=eng)
                    else:
                        ts(dst[:, lo:hi], src[:, lo:hi], wcol(half), first_bias, OP.mult, OP.add, [sB, Bv], [dB], eng=eng)
                    m = "seq" if seg == "ctx" else mode
                    for k in range(ntap):
                        s = k - half
                        if s == 0:
                            continue
                        if m == "seq":
                            n = hi - lo
                            a0, a1 = max(0, -s), n - max(0, s)
                            if a1 <= a0:
                                continue
                            o_ap = dst[:, lo + a0:lo + a1]
                            i_ap = src[:, lo + a0 + s:lo + a1 + s]
                        elif m == "h":
                            dv_ = dst[:, lo:hi].rearrange("p (r c) -> p r c", c=64)
                            sv_ = src[:, lo:hi].rearrange("p (r c) -> p r c", c=64)
                            a0, a1 = max(0, -s), 64 - max(0, s)
                            o_ap = dv_[:, :, a0:a1]
                            i_ap = sv_[:, :, a0 + s:a1 + s]
                        else:
                            r0, r1 = max(0, -s), R - max(0, s)
                            if r1 <= r0:
                                continue
                            o_ap = dst[:, lo + 64 * r0:lo + 64 * r1]
                            i_ap = src[:, lo + 64 * (r0 + s):lo + 64 * (r1 + s)]
                        stt(o_ap, i_ap, wcol(k), o_ap, OP.mult, OP.add, [sB, dB, Bv], [dB], eng=eng)

            BAs = {"ctx": BA, "lat": BA}
            BBs = {"ctx": BB, "lat": BB}
            allA = [BA]
            allB = [BB]
            for ct in range(4):
                dma(BIGA[:, :], Ud[ct * P:(ct + 1) * P, :], [B("Ud")], allA)
                dwconv(BIGA, BAs, BIGB, BBs, lambda k, ct=ct: V(("convw", l), ct * 31 + k), 31,
                       V(("convb", l), ct), "h" if ct < 2 else "v", do_ctx=not last)
                dma(CVd[ct * P:(ct + 1) * P, :], BIGB[:, :], allB, [B("CVd")], eng="pool", key="st_BIGB")

            if STOP < 4:
                break
            psum_fence()
            Ov = BIGB[:, :].rearrange("p (t d) -> p t d", d=P)
            fwd_order = list(range(NT))
            bwd_order = [1, 0] + list(range(NT - 1, 1, -1))
            for hd in range(4):
                for m_i, dstT in ((0, qT), (1, kT), (2, VTt)):
                    if m_i >= int(os.environ.get('KSUB', '3')):
                        continue
                    row0 = m_i * 512 + hd * P
                    dma(BIGA[:, :], QKVd[row0:row0 + P, :], [B("QKVd")], allA)
                    dwconv(BIGA, BAs, BIGB, BBs, lambda k, m_i=m_i: V(("scw", l), (m_i * 4 + hd) * 5 + k), 5, None, "seq")
                    act(BIGB[:, :], BIGB[:, :], AF.Silu, [BB], allB)
                    if m_i < 2:
                        for t0 in range(0, T, NB):
                            si = nxt("sq", 2)
                            act(sqb[si][:, :], BIGB[:, t0:t0 + NB], AF.Square, [BB], [B("sqb%d" % si)])
                            mm(pS[:, 0:NB], onesb, sqb[si][:, :], [Bcb, B("sqb%d" % si)], [B("pS")])
                            act(rstd[:, :], pS[:, 0:NB], AF.Sqrt, [B("pS"), Beps], [B("rstd")], bias=EPS6, scale=1.0)
                            recip(rstd[:, :], rstd[:, :], [B("rstd")], [B("rstd")])
                            if m_i == 0:
                                stt(dstT[:, t0:t0 + NB], BIGB[:, t0:t0 + NB], float(P) ** -0.5, rstd[:, :], OP.mult, OP.mult, [BB, B("rstd")], [Bwin])
                            else:
                                tt(dstT[:, t0:t0 + NB], BIGB[:, t0:t0 + NB], rstd[:, :], OP.mult, [BB, B("rstd")], [Bwin])
                    else:
                        for ti in range(NT):
                            ps_, psB = pM[:, 384:512], B("pM")
                            si = nxt("sq", 2)
                            cp(sqb[si][:, 0:P], BIGB[:, ti * P:(ti + 1) * P], [BB], [B("sqb%d" % si)], eng="act")
                            mm(ps_, sqb[si][:, 0:P], identb, [B("sqb%d" % si), Bcb], [psB])
                            cp(VTt[:, ti * P:(ti + 1) * P], ps_, [psB], [Bwin], eng="dve")

                if STOP < 5:
                    continue
                def big_fence():
                    ws = [BA, B("fz")] + [B("%s_%d" % (n, sl)) for sl in range(2, 2 * NSTEP) for n in DTF + DTB]
                    A.op("pool", lambda e: e.memset(fz[:, :], 0.0), writes=ws)

                big_fence()
                for dr in range(2):
                    A.op("pool", lambda e, dr=dr: e.memset(ST[("Sf", dr)], 0.0), writes=[B("Sf%d" % dr)])
                    A.op("pool", lambda e, dr=dr: e.memset(ST[("Sb", dr)], 0.0), writes=[B("Sb%d" % dr)])
                seen = set()

                def setup_ops(sl, dr, ti):
                    d = lambda n: DT[(n, sl)]
                    bb = lambda n: B("%s_%d" % (n, sl))
                    tsl = slice(ti * P, (ti + 1) * P)
                    pKK, pKKB = pd(0 + 6 * sl)
                    pQK, pQKB = pd(1 + 6 * sl)
                    pZ, pZB = pd(2 + 6 * sl)
                    p3, p3B = pd(3 + 6 * sl)
                    p4, p4B = pd(4 + 6 * sl)
                    p5, p5B = pd(5 + 6 * sl)
                    gc = SCL("GC", dr, hd, ti)
                    ops = []
                    ops.append(lambda: (mm(pKK, kT[:, tsl], kT[:, tsl], [Bwin], [pKKB]),
                                        ts(d("Gd"), ident, gc, None, OP.mult, None, [Bc, BS], [bb("Gd")], eng="pool")))
                    ops.append(lambda: (mm(pZ, ones, d("Gd"), [Bc, bb("Gd")], [pZB], start=True, stop=False),
                                        mm(pZ, ident, pen[dr], [Bc], [pZB], start=False, stop=True)))
                    ops.append(lambda: (ts(d("D"), pZ, gc, None, OP.subtract, None, [pZB, BS], [bb("D")]),
                                        mm(pQK, qT[:, tsl], kT[:, tsl], [Bwin], [pQKB])))
                    ops.append(lambda: act(d("D"), d("D"), AF.Exp, [bb("D")], [bb("D")], scale=-1.0))
                    ops.append(lambda: tt(d("DMs"), d("D"), msk[dr], OP.mult, [bb("D"), Bc], [bb("DMs")], eng="pool"))
                    ops.append(lambda: (stt(d("N0"), pKK, SCL("NBE", dr, hd, ti), d("DMs"), OP.mult, OP.mult, [pKKB, BS, bb("DMs")], [bb("N0")]),
                                        tt(d("attn"), pQK, d("D"), OP.mult, [pQKB, bb("D")], [bb("attn")])))
                    ops.append(lambda: (mm(p4, d("N0"), ident, [bb("N0"), Bc], [p4B]),
                                        mm(p3, d("attn"), identb, [bb("attn"), Bcb], [p3B]),
                                        mm(p5, kT[:, tsl], identb, [Bwin, Bcb], [p5B])))
                    ops.append(lambda: (cp(d("U0"), p4, [p4B], [bb("U0")], eng="act"),
                                        cp(d("attnT"), p3, [p3B], [bb("attnT")], eng="act"),
                                        ts(d("kbg"), p5, SCL("BKG", dr, hd, ti), None, OP.mult, None, [p5B, BS], [bb("kbg")])))
                    ops.append(lambda: (tt(d("Y"), d("U0"), ident, OP.add, [bb("U0"), Bc], [bb("Y")], eng="pool"),
                                        ts(d("kt"), p5, SCL("EKT", dr, hd, ti), None, OP.mult, None, [p5B, BS], [bb("kt")]),
                                        ts(d("vb"), VTt[:, tsl], SCL("BE", dr, hd, ti), None, OP.mult, None, [Bwin, BS], [bb("vb")], eng="pool")))
                    return ops

                assert NT % NSTEP == 0
                for pair in range(NT // NSTEP):
                    chains = []
                    for j in range(NSTEP):
                        step = NSTEP * pair + j
                        chains.append((2 * j + 0, 0, fwd_order[step]))
                        chains.append((2 * j + 1, 1, bwd_order[step]))
                    allops = [setup_ops(sl, dr, ti) for sl, dr, ti in chains]
                    for i in range(len(allops[0])):
                        for ops in allops:
                            ops[i]()
                    cur = {sl: ("N0", "U0") for sl, _, _ in chains}
                    for lvl in range(1, 7):
                        info = []
                        for sl, dr, ti in chains:
                            d = lambda n, sl=sl: DT[(n, sl)]
                            bb = lambda n, sl=sl: B("%s_%d" % (n, sl))
                            nN, nU = cur[sl]
                            oN, oU = ("N1", "U1") if nN == "N0" else ("N0", "U0")
                            pa_, paB = pd(3 + 6 * sl)
                            pb_, pbB = pd(4 + 6 * sl)
                            mm(pb_, d(nU), d(nN), [bb(nU), bb(nN)], [pbB])
                            if lvl < 6:
                                mm(pa_, d(nN), d(nU), [bb(nU), bb(nN)], [paB])
                            info.append((sl, d, bb, oN, oU, pa_, paB, pb_, pbB))
                            cur[sl] = (oN, oU)
                        for sl, d, bb, oN, oU, pa_, paB, pb_, pbB in info:
                            cp(d(oN), pb_, [pbB], [bb(oN)], eng=("act" if sl % 2 else "dve"))
                            if lvl < 6:
                                cp(d(oU), pa_, [paB], [bb(oU)], eng=("dve" if sl % 2 else "act"))
                        for sl, d, bb, oN, oU, pa_, paB, pb_, pbB in info:
                            pc_, pcB = pd(5 + 6 * sl)
                            mm(pc_, d(oN), d("Y"), [bb(oN), bb("Y")], [pcB])
                        for sl, d, bb, oN, oU, pa_, paB, pb_, pbB in info:
                            pc_, pcB = pd(5 + 6 * sl)
                            tt(d("Y"), d("Y"), pc_, OP.add, [pcB, bb("Y")], [bb("Y")])
                    for sl, dr, ti in chains:
                        d = lambda n, sl=sl: DT[(n, sl)]
                        bb = lambda n, sl=sl: B("%s_%d" % (n, sl))
                        cp(d("TTb"), d("Y"), [bb("Y")], [bb("TTb")], eng="act")
                    for sl, dr, ti in chains:
                        d = lambda n, sl=sl: DT[(n, sl)]
                        bb = lambda n, sl=sl: B("%s_%d" % (n, sl))
                        pa_, paB = pd(3 + 6 * sl)
                        pb_, pbB = pd(4 + 6 * sl)
                        mm(pa_, d("TTb"), d("vb"), [bb("TTb"), bb("vb")], [paB])
                        mm(pb_, d("kbg"), d("TTb"), [bb("kbg"), bb("TTb")], [pbB])
                    for sl, dr, ti in chains:
                        d = lambda n, sl=sl: DT[(n, sl)]
                        bb = lambda n, sl=sl: B("%s_%d" % (n, sl))
                        pa_, paB = pd(3 + 6 * sl)
                        pb_, pbB = pd(4 + 6 * sl)
                        cp(d("u"), pa_, [paB], [bb("u")], eng="act")
                        cp(d("wT"), pb_, [pbB], [bb("wT")], eng="dve")
                    for j in range(NSTEP):
                        cj = [c for c in chains if c[0] // 2 == j]
                        for sl, dr, ti in cj:
                            d = lambda n, sl=sl: DT[(n, sl)]
                            bb = lambda n, sl=sl: B("%s_%d" % (n, sl))
                            Sf, Sbb = ST[("Sf", dr)], ST[("Sb", dr)]
                            SfB, SbB = B("Sf%d" % dr), B("Sb%d" % dr)
                            tsl = slice(ti * P, (ti + 1) * P)
                            pc_, pcB = pd(3 + 6 * sl)
                            pe_, peB = pd(0 + 6 * sl)
                            pf_, pfB = pd(1 + 6 * sl)
                            pg_, pgB = pd(2 + 6 * sl)
                            mm(pc_, d("wT"), Sbb, [bb("wT"), SbB], [pcB])
                            mm(pe_, qT[:, tsl], Sbb, [Bwin, SbB], [peB])
                            tt(d("vn"), d("u"), pc_, OP.subtract, [bb("u"), pcB], [bb("vn")])
                            mm(pg_, d("kt"), d("vn"), [bb("kt"), bb("vn")], [pgB])
                            mm(pf_, d("attnT"), d("vn"), [bb("attnT"), bb("vn")], [pfB])
                            stt(Sf, Sf, SCL("GL", dr, hd, ti), pg_, OP.mult, OP.add, [SfB, BS, pgB], [SfB])
                            cp(Sbb, Sf, [SfB], [SbB], eng="act")
                            ts(d("t"), pe_, SCL("EG", dr, hd, ti), None, OP.mult, None, [peB, BS], [bb("t")])
                            if ti in seen:
                                tt(Ov[:, ti, :], Ov[:, ti, :], d("t"), OP.add, [bb("t"), BB], [BB])
                                tt(Ov[:, ti, :], Ov[:, ti, :], pf_, OP.add, [pfB, BB], [BB])
                            else:
                                tt(Ov[:, ti, :], d("t"), pf_, OP.add, [bb("t"), pfB], [BB])
                                seen.add(ti)
                big_fence()

                if int(os.environ.get('KSC', '3')) < 3 or int(os.environ.get('KOUT', '1')) < 1:
                    continue
                do_out_ctx = not last
                act(BIGA[:, :], BIGB[:, :], AF.Square, [BB], allA)
                A.op("dve", lambda e: e.tensor_reduce(out=orstd[:, :], in_=BIGA[:, :].rearrange("p (t d) -> p t d", d=P), axis=AX.X, op=OP.add), reads=[BA], writes=[B("orstd")])
                act(orstd[:, :], orstd[:, :], AF.Sqrt, [B("orstd"), Beps], [B("orstd")], bias=EPS6, scale=1.0 / P)
                recip(orstd[:, :], orstd[:, :], [B("orstd")], [B("orstd")])
                dma(BIGA[:, :], ZGd[hd * P:(hd + 1) * P, :], [B("ZGd")], allA)
                for ti in range(0 if do_out_ctx else 2, NT):
                    stt(ont[:, :], Ov[:, ti, :], orstd[:, ti:ti + 1], V(("dnw", l), 0, P), OP.mult, OP.mult, [BB, B("orstd"), Bv], [B("ont")])
                    ps_, psB = pM[:, 384:512], B("pM")
                    mm(ps_, ont[:, :], ident, [B("ont"), Bc], [psB])
                    tt(ydt[:, :], ps_, BIGA[:, ti * P:(ti + 1) * P], OP.mult, [psB, BA], [B("ydt")])
                    dma(YDd[hd * P:(hd + 1) * P, ti * P:(ti + 1) * P], ydt[:, :], [B("ydt")], [B("YDd")], eng="pool", key="st_ydt")

            if STOP < 6:
                break
            psum_fence()
            for k in range(KC):
                load_weight_bf16(woutb, BA, w_out[l, k * P:(k + 1) * P, :], DM, coloff=k * DM)
            cvv = cvb[:, :].rearrange("p (k n) -> p k n", k=4)
            cgv = cgb[:, :].rearrange("p (k n) -> p k n", k=4)
            for (t0, nb, c) in blocks:
                if last and c == 0:
                    continue
                dma(cvv[:, :, 0:nb], CVd.rearrange("(k p) t -> p k t", p=P)[:, :, t0:t0 + nb], [B("CVd")], [B("cvb")])
                dma(cgv[:, :, 0:nb], CGd.rearrange("(k p) t -> p k t", p=P)[:, :, t0:t0 + nb], [B("CGd")], [B("cgb")])
                dma(hv[:, 4:8, 0:nb], YDd.rearrange("(k p) t -> p k t", p=P)[:, :, t0:t0 + nb], [B("YDd")], [Bh])
                dma(xv[:, :, 0:nb], xsrc.rearrange("(k p) t -> p k t", p=P)[:, :, t0:t0 + nb], [xsrcB], [Bx])
                for ct in range(4):
                    mm(pS[:, 0:nb], ones, cvv[:, ct, 0:nb], [Bc, B("cvb")], [B("pS")], start=(ct == 0), stop=(ct == 3))
                act(mean[:, 0:nb], pS[:, 0:nb], AF.Copy, [B("pS")], [B("mean")], scale=1.0 / 512)
                for ct in range(4):
                    i = nxt("ev", 4)
                    act(ev[i][:, 0:nb], cvv[:, ct, 0:nb], AF.Square, [B("cvb")], [B("ev%d" % i)])
                    mm(pM[:, 0:nb], ones, ev[i][:, 0:nb], [Bc, B("ev%d" % i)], [B("pM")], start=(ct == 0), stop=(ct == 3))
                i = nxt("ev", 4)
                tt(ev[i][:, 0:nb], mean[:, 0:nb], mean[:, 0:nb], OP.mult, [B("mean")], [B("ev%d" % i)])
                stt(rstd[:, 0:nb], pM[:, 0:nb], 1.0 / 512, ev[i][:, 0:nb], OP.mult, OP.subtract, [B("pM"), B("ev%d" % i)], [B("rstd")])
                act(rstd[:, 0:nb], rstd[:, 0:nb], AF.Sqrt, [B("rstd"), Beps], [B("rstd")], bias=EPS5, scale=1.0)
                recip(rstd[:, 0:nb], rstd[:, 0:nb], [B("rstd")], [B("rstd")])
                for ct in range(4):
                    i = nxt("ev", 4)
                    tt(ev[i][:, 0:nb], cvv[:, ct, 0:nb], mean[:, 0:nb], OP.subtract, [B("cvb"), B("mean")], [B("ev%d" % i)])
                    tt(ev[i][:, 0:nb], ev[i][:, 0:nb], rstd[:, 0:nb], OP.mult, [B("ev%d" % i), B("rstd")], [B("ev%d" % i)])
                    act(ev[i][:, 0:nb], ev[i][:, 0:nb], AF.Silu, [B("ev%d" % i), Bv], [B("ev%d" % i)], bias=V(("lnb", l), ct), scale=V(("lnw", l), ct))
                    tt(hv[:, ct, 0:nb], ev[i][:, 0:nb], cgv[:, ct, 0:nb], OP.mult, [B("ev%d" % i), B("cgb")], [Bh])
                for fc in range(KC):
                    pi = nxt("pA", 2)
                    for k in range(KC):
                        mm(pA[pi][:, 0:nb], woutb[:, k * DM + fc * P:k * DM + (fc + 1) * P], hv[:, k, 0:nb], [BA, Bh], [B("pA%d" % pi)], start=(k == 0), stop=(k == KC - 1))
                    stt(xv[:, fc, 0:nb], pA[pi][:, 0:nb], GATE(fc, c), xv[:, fc, 0:nb], OP.mult, OP.add, [B("pA%d" % pi), Bmod, Bx], [Bx])
                if not last:
                    dma(XS.rearrange("(k p) t -> p k t", p=P)[:, :, t0:t0 + nb], xv[:, :, 0:nb], [Bx], [B("XS")], eng="pool", key="st_xb")
                else:
                    rms_stats(lambda k: xv[:, k, 0:nb], Bx, nb, rstd, 1.0 / DM, EPS6)
                    for k in range(KC):
                        stt(xv[:, k, 0:nb], xv[:, k, 0:nb], V("fnw", k), rstd[:, 0:nb], OP.mult, OP.mult, [Bx, Bv, B("rstd")], [Bx])
                    dma(outT.rearrange("(k p) t -> p k t", p=P)[:, :, t0 - TC:t0 - TC + nb], xv[:, :, 0:nb], [Bx], [B("outT")], eng="pool", key="st_xb")

        A.final_waits = ["st_xb"]
        A.emit(nc)
    return nc


def make_consts():
    i = np.arange(P)
    c = np.zeros((P, 8, P), np.float32)
    c[:, 0] = np.eye(P)
    c[:, 1] = 1.0
    c[:, 2] = (i[:, None] <= i[None, :])
    c[:, 3] = (i[:, None] >= i[None, :])
    c[:, 4] = np.where(i[:, None] >= i[None, :], 0.0, BIG)
    c[:, 5] = np.where(i[:, None] <= i[None, :], 0.0, BIG)
    c[:, 6] = (i[:, None] > i[None, :])
    c[:, 7] = (i[:, None] < i[None, :])
    return c.reshape(P, 8 * P)


def make_vec(L, b, c, c_ctx, norm_w, b_ada, conv_w, conv_b, conv_ln_w, conv_ln_b, short_conv_w, a_log,
             dt_bias, dn_norm_w, final_norm_w):
    voff, NV = vec_layout(L)
    v = np.zeros((P, NV), np.float32)
    pk = lambda a, n: np.asarray(a, np.float32).reshape(n, P).T
    for l in range(L):
        v[:, voff[("normw", l)]:voff[("normw", l)] + 8] = pk(norm_w[l], 8)
        v[:, voff[("bada", l)]:voff[("bada", l)] + 24] = pk(b_ada[l], 24)
        cw = np.asarray(conv_w[l], np.float32)
        v[:, voff[("convw", l)]:voff[("convw", l)] + 124] = cw.reshape(31, 4, P).transpose(2, 1, 0).reshape(P, 124)
        v[:, voff[("convb", l)]:voff[("convb", l)] + 4] = pk(conv_b[l], 4)
        v[:, voff[("lnw", l)]:voff[("lnw", l)] + 4] = pk(conv_ln_w[l], 4)
        v[:, voff[("lnb", l)]:voff[("lnb", l)] + 4] = pk(conv_ln_b[l], 4)
        sw = np.asarray(short_conv_w[l], np.float32)
        v[:, voff[("scw", l)]:voff[("scw", l)] + 60] = sw.reshape(5, 12, P).transpose(2, 1, 0).reshape(P, 60)
        v[:, voff[("alog", l)]:voff[("alog", l)] + 8] = np.asarray(a_log[l], np.float32).reshape(1, 8)
        v[:, voff[("dtb", l)]:voff[("dtb", l)] + 8] = np.asarray(dt_bias[l], np.float32).reshape(1, 8)
        v[:, voff[("dnw", l)]:voff[("dnw", l)] + P] = np.asarray(dn_norm_w[l], np.float32).reshape(1, P)
    v[:, voff["fnw"]:voff["fnw"] + 8] = pk(final_norm_w, 8)
    cond = np.stack([pk(c_ctx, 8), pk(c[b], 8)], axis=-1).reshape(P, 16)
    v[:, voff["cond"]:voff["cond"] + 16] = cond
    return v


_prog_cache = {}


def kernel(x, c, ctx, c_ctx, norm_w, w_ada, b_ada, w_in, conv_w, conv_b, conv_ln_w, conv_ln_b,
           short_conv_w, a_log, dt_bias, dn_norm_w, w_out, final_norm_w):
    x = np.asarray(x, np.float32)
    ctx = np.asarray(ctx, np.float32)
    Bsz, SEQ, _ = x.shape
    L = np.asarray(norm_w).shape[0]
    R = SEQ // 64
    key = (R, L)
    if key not in _prog_cache:
        _prog_cache[key] = build_program(R, L)
    nc = _prog_cache[key]
    cstv = make_consts()
    w_ada_f = np.ascontiguousarray(np.asarray(w_ada, np.float32))
    w_in_f = np.ascontiguousarray(np.asarray(w_in, np.float32))
    w_out_f = np.ascontiguousarray(np.asarray(w_out, np.float32))
    in_maps = []
    for b in range(Bsz):
        xTb = np.ascontiguousarray(np.concatenate([ctx[b], x[b]], axis=0).T)
        vecb = make_vec(L, b, np.asarray(c, np.float32), np.asarray(c_ctx, np.float32), norm_w, b_ada, conv_w, conv_b,
                        conv_ln_w, conv_ln_b, short_conv_w, a_log, dt_bias, dn_norm_w, final_norm_w)
        in_maps.append({"xT": xTb, "vec_in": vecb, "cst_in": cstv, "w_ada": w_ada_f, "w_in": w_in_f, "w_out": w_out_f})
    res = run_bass_kernel_spmd(nc, in_maps, core_ids=list(range(Bsz)))
    out = np.stack([np.ascontiguousarray(r["outT"].T) for r in res.results], axis=0)
    return out.astype(np.float32)
```

```python
import os
import numpy as np
from contextlib import ExitStack
import concourse.bass as bass
import concourse.mybir as mybir
from concourse.bass_utils import run_bass_kernel_spmd

F32 = mybir.dt.float32
BF16 = mybir.dt.bfloat16
AF = mybir.ActivationFunctionType
OP = mybir.AluOpType
AX = mybir.AxisListType

P = 128
DM = 1024
KC = 8
DIN = 3600
TC = 256
NB = 256
BIG = 1.0e4
ENGS = ("pe", "act", "dve", "pool", "sp")
STOP = int(os.environ.get("KSTOP", "9"))


class Buf:
    __slots__ = ("name", "last_w", "readers", "excl")

    def __init__(self, name, excl=False):
        self.name, self.last_w, self.readers, self.excl = name, None, [], excl


class Node:
    __slots__ = ("eng", "fn", "deps", "dma", "key", "signal", "cnt", "ep")

    def __init__(self, eng, fn, deps, dma, key):
        self.eng, self.fn, self.deps, self.dma, self.key = eng, fn, deps, dma, key
        self.signal, self.cnt, self.ep = dma, 0, 0


class Asm:
    def __init__(self):
        self.ins = {e: [] for e in ENGS}
        self.dma_keys = {}
        self.final_waits = []

    def op(self, eng, fn, reads=(), writes=(), dma=False, key=None):
        ex = [b for b in reads if b.excl]
        if ex:
            reads = [b for b in reads if not b.excl]
            writes = list(writes) + [b for b in ex if b not in writes]
        deps = []
        for b in reads:
            if b.last_w is not None:
                deps.append(b.last_w)
        for b in writes:
            if b.last_w is not None:
                deps.append(b.last_w)
            deps.extend(b.readers)
        if dma:
            if key is None:
                key = writes[0].name
            self.dma_keys.setdefault(key, 0)
        node = Node(eng, fn, deps, dma, key)
        for b in reads:
            b.readers.append(node)
        for b in writes:
            b.last_w = node
            b.readers = []
        self.ins[eng].append(node)
        return node

    def emit(self, nc):
        for e in ENGS:
            for n in self.ins[e]:
                for d in n.deps:
                    if d is n or d.dma:
                        continue
                    if d.eng == "pe" and n.eng == "pe" and not n.dma:
                        continue
                    d.signal = True
        keycnt = {k: 0 for k in self.dma_keys}
        EPOCH = int(os.environ.get("KEPOCH", "16000"))
        nep = {e: 1 for e in ENGS}
        for e in ENGS:
            c = 0
            ep = 0
            for n in self.ins[e]:
                if n.dma:
                    keycnt[n.key] += 16
                    n.cnt = keycnt[n.key]
                elif n.signal:
                    if c >= EPOCH:
                        c = 0
                        ep += 1
                    c += 1
                    n.cnt, n.ep = c, ep
            nep[e] = ep + 1
        assert max(keycnt.values() or [0]) < 32000
        with ExitStack() as st:
            esem = {(e, i): st.enter_context(nc.semaphore("sem_%s%d" % (e, i))) for e in ENGS if e != "sp" for i in range(nep[e])}
            ksem = {k: st.enter_context(nc.semaphore("dk_%d" % i)) for i, k in enumerate(self.dma_keys)}
            block = st.enter_context(nc.Block())
            lastk = {}
            for e in ENGS:
                for n in self.ins[e]:
                    if n.dma and n.key in self.final_waits:
                        lastk[n.key] = max(lastk.get(n.key, 0), n.cnt)

            def run(e, eng):
                waited = {}
                for n in self.ins[e]:
                    for d in n.deps:
                        if d is n:
                            continue
                        if d.dma:
                            s, v, kk = ksem[d.key], d.cnt, ("k", d.key)
                        else:
                            if d.eng == "pe" and e == "pe" and not n.dma:
                                continue
                            s, v, kk = esem[(d.eng, d.ep)], d.cnt, ("e", d.eng, d.ep)
                        if waited.get(kk, 0) >= v:
                            continue
                        waited[kk] = v
                        eng.wait_ge(s, v)
                    ins = n.fn(eng)
                    if n.dma:
                        ins.then_inc(ksem[n.key], 16)
                    elif n.signal:
                        ins.then_inc(esem[(e, n.ep)], 1)
                if e == "sp":
                    for k, v in lastk.items():
                        eng.wait_ge(ksem[k], v)

            block.tensor(lambda eng: run("pe", eng))
            block.scalar(lambda eng: run("act", eng))
            block.vector(lambda eng: run("dve", eng))
            block.gpsimd(lambda eng: run("pool", eng))
            block.sync(lambda eng: run("sp", eng))


def vec_layout(L):
    off = {}
    o = 0
    per = [("normw", 8), ("bada", 24), ("convw", 124), ("convb", 4), ("lnw", 4), ("lnb", 4),
           ("scw", 60), ("alog", 8), ("dtb", 8), ("dnw", 128)]
    for l in range(L):
        for nm, n in per:
            off[(nm, l)] = o
            o += n
    off["fnw"] = o
    o += 8
    off["cond"] = o
    o += 16
    return off, o


def build_program(R, L):
    TL = 64 * R
    T = TC + TL
    NT = T // P
    assert TL % NB == 0
    blocks = [(0, NB, 0)] + [(TC + i * NB, NB, 1) for i in range(TL // NB)]
    voff, NV = vec_layout(L)

    nc = bass.Bass("TRN2", target_bir_lowering=False)
    d_in = lambda n, s: nc.dram_tensor(n, s, F32, kind="ExternalInput").ap()
    xT = d_in("xT", [DM, T])
    vec_d = d_in("vec_in", [P, NV])
    cst_d = d_in("cst_in", [P, 8 * P])
    w_ada = d_in("w_ada", [L, DM, 3 * DM])
    w_in = d_in("w_in", [L, DM, DIN])
    w_out = d_in("w_out", [L, DM, DM])
    outT = nc.dram_tensor("outT", [DM, TL], F32, kind="ExternalOutput").ap()
    XS = nc.dram_tensor("XS", [DM, T], F32).ap()
    Ud = nc.dram_tensor("Ud", [512, T], F32).ap()
    CGd = nc.dram_tensor("CGd", [512, T], F32).ap()
    QKVd = nc.dram_tensor("QKVd", [1536, T], F32).ap()
    ZGd = nc.dram_tensor("ZGd", [512, T], F32).ap()
    CVd = nc.dram_tensor("CVd", [512, T], F32).ap()
    YDd = nc.dram_tensor("YDd", [512, T], BF16).ap()

    A = Asm()
    bufs = {}

    def B(name):
        if name not in bufs:
            bufs[name] = Buf(name)
        return bufs[name]

    with ExitStack() as st:
        sb = lambda n, s, d=F32: st.enter_context(nc.sbuf_tensor(n, s, d))
        pst = lambda n, s: st.enter_context(nc.psum_tensor(n, s, F32))
        winb = sb("winb", [P, KC * DIN], BF16)
        BIGA = sb("BIGA", [P, T])
        BIGB = sb("BIGB", [P, T])
        wst = sb("wst", [P, 1024])
        xb = sb("xb", [P, KC * NB])
        hb = sb("hb", [P, KC * NB], BF16)
        sqb = [sb("sqb%d" % i, [P, NB], BF16) for i in range(2)]
        ev = [sb("ev%d" % i, [P, NB]) for i in range(4)]
        cvb = sb("cvb", [P, 4 * NB])
        cgb = sb("cgb", [P, 4 * NB])
        rstd = sb("rstd", [P, NB])
        mean = sb("mean", [P, NB])
        vec = sb("vec", [P, NV])
        cst = sb("cst", [P, 8 * P])
        cstb = sb("cstb", [P, 2 * P], BF16)
        epsc = sb("epsc", [P, 4])
        sc = sb("sc", [P, 16])
        modt = sb("modt", [P, 48])
        gwt = sb("gwt", [P, 16])
        ABT = sb("ABT", [P, NT * 16])
        SC = {n: sb("S_" + n, [P, 8 * NT]) for n in ("G", "GC", "GT", "EG", "GL", "BE", "NBE")}
        SC["EKT"] = SC["GT"]
        SC["BKG"] = SC["G"]
        eal = sb("eal", [P, 8])
        tmpc = sb("tmpc", [P, NT])
        tmpc2 = sb("tmpc2", [P, NT])
        orstd = sb("orstd", [P, NT])
        DTF = ("Gd", "D", "DMs", "N0", "U0", "N1", "U1", "Y", "u", "t")
        DTB = ("attn", "attnT", "TTb", "wT", "kbg", "kt", "vb", "vn")
        DT = {}
        for sl in range(2):
            for n in DTF:
                DT[(n, sl)] = sb("d_%s%d" % (n, sl), [P, P])[:, :]
            for n in DTB:
                DT[(n, sl)] = sb("d_%s%d" % (n, sl), [P, P], BF16)[:, :]
        off = 0
        NSTEP = 3
        need = (2 * NSTEP - 2) * (len(DTF) * P + len(DTB) * P // 2)
        carve = BIGA if need <= T else sb("dtx", [P, need])
        for sl in range(2, 2 * NSTEP):
            for n in DTF:
                DT[(n, sl)] = carve[:, off:off + P]
                off += P
            for n in DTB:
                DT[(n, sl)] = carve[:, off:off + P // 2].bitcast(BF16)
                off += P // 2
        ST = {}
        for dr in range(2):
            ST[("Sf", dr)] = sb("d_Sf%d" % dr, [P, P])[:, :]
            ST[("Sb", dr)] = sb("d_Sb%d" % dr, [P, P], BF16)[:, :]
        fz = sb("fz", [P, 1])
        ont = sb("ont", [P, P])
        ydt = sb("ydt", [P, P], BF16)

        bank = [pst("bank%d" % i, [P, 512]) for i in range(8)]
        for i in range(8):
            bufs["bank%d" % i] = Buf("bank%d" % i, excl=True)
        for nm, i in (("pA0", 0), ("pA1", 1), ("pS", 2), ("pM", 3)):
            bufs[nm] = bufs["bank%d" % i]
        pA = [bank[0], bank[1]]
        pS, pM = bank[2], bank[3]
        PHYS = {0: 0, 1: 1, 2: 2, 3: 3, 4: 2, 5: 0}

        def pd(i):
            ch_, lg_ = i // 6, i % 6
            c0 = PHYS[lg_] * P
            return bank[ch_][:, c0:c0 + P], bufs["bank%d" % ch_]

        def psum_fence():
            pass

        ident = cst[:, 0:P]
        ones = cst[:, P:2 * P]
        tri = [cst[:, 2 * P:3 * P], cst[:, 3 * P:4 * P]]
        pen = [cst[:, 4 * P:5 * P], cst[:, 5 * P:6 * P]]
        msk = [cst[:, 6 * P:7 * P], cst[:, 7 * P:8 * P]]
        identb = cstb[:, 0:P]
        onesb = cstb[:, P:2 * P]
        V = lambda key, i=0, n=1: vec[:, voff[key] + i: voff[key] + i + n]

        woutb = BIGA[:, :].bitcast(BF16) if 2 * T >= KC * DM else sb("woutb", [P, KC * DM], BF16)[:, :]
        qT = winb[:, 0:T]
        kT = winb[:, T:2 * T]
        VTt = winb[:, 2 * T:3 * T]
        Bwin, BA, BB = B("winb"), B("BIGA"), B("BIGB")

        def dma(out, in_, reads, writes, eng="sp", key=None):
            A.op(eng, lambda e: e.dma_start(out=out, in_=in_), reads=reads, writes=writes, dma=True, key=key)

        def act(out, in_, func, reads, writes, bias=None, scale=None):
            kw = {}
            if bias is not None:
                kw["bias"] = bias
            if scale is not None:
                kw["scale"] = scale
            A.op("act", lambda e: e.activation(out=out, in_=in_, func=func, **kw), reads=reads, writes=writes)

        def mm(out, lhsT, rhs, reads, writes, start=True, stop=True):
            A.op("pe", lambda e: e.matmul(out, lhsT=lhsT, rhs=rhs, start=start, stop=stop), reads=reads, writes=writes)

        def tt(out, in0, in1, op, reads, writes, eng="dve"):
            A.op(eng, lambda e: e.tensor_tensor(out=out, in0=in0, in1=in1, op=op), reads=reads, writes=writes)

        def ts(out, in0, s1, s2, op0, op1, reads, writes, eng="dve"):
            if s2 is None:
                A.op(eng, lambda e: e.tensor_scalar(out=out, in0=in0, scalar1=s1, scalar2=None, op0=op0), reads=reads, writes=writes)
            else:
                A.op(eng, lambda e: e.tensor_scalar(out=out, in0=in0, scalar1=s1, scalar2=s2, op0=op0, op1=op1), reads=reads, writes=writes)

        def stt(out, in0, s, in1, op0, op1, reads, writes, eng="dve"):
            A.op(eng, lambda e: e.scalar_tensor_tensor(out=out, in0=in0, scalar=s, in1=in1, op0=op0, op1=op1), reads=reads, writes=writes)

        def cp(out, in_, reads, writes, eng="dve"):
            if eng == "act":
                act(out, in_, AF.Copy, reads, writes)
            else:
                A.op(eng, lambda e: e.tensor_copy(out=out, in_=in_), reads=reads, writes=writes)

        def recip(out, in_, reads, writes):
            A.op("dve", lambda e: e.reciprocal(out=out, in_=in_), reads=reads, writes=writes)

        Bv, Bc = B("vec"), B("cst")
        dma(vec[:, :], vec_d[:, :], [], [Bv])
        dma(cst[:, :], cst_d[:, :], [], [Bc])
        cp(cstb[:, :], cst[:, 0:2 * P], [Bc], [B("cstb")])
        Bcb = B("cstb")
        Beps = B("epsc")
        A.op("pool", lambda e: e.memset(epsc[:, 0:1], 1e-6), writes=[Beps])
        A.op("pool", lambda e: e.memset(epsc[:, 1:2], 1e-5), writes=[Beps])
        A.op("pool", lambda e: e.memset(epsc[:, 2:3], 1.0), writes=[Beps])
        A.op("pool", lambda e: e.memset(epsc[:, 3:4], 0.0), writes=[Beps])
        EPS6, EPS5, ONE1 = epsc[:, 0:1], epsc[:, 1:2], epsc[:, 2:3]
        act(sc[:, :], V("cond", 0, 16), AF.Silu, [Bv], [B("sc")])

        rr = {"ev": 0, "pA": 0, "sq": 0}

        def nxt(k, n):
            i = rr[k]
            rr[k] = (i + 1) % n
            return i

        def load_weight_bf16(dst, dstB, src_rows, ncols, coloff=0):
            c0 = 0
            i = 0
            while c0 < ncols:
                w = min(1024, ncols - c0)
                dma(wst[:, 0:w], src_rows[:, c0:c0 + w], [], [B("wst")])
                eng = ("act", "dve", "pool")[i % 3]
                cp(dst[:, coloff + c0: coloff + c0 + w], wst[:, 0:w], [B("wst")], [dstB], eng=eng)
                c0 += w
                i += 1

        def rms_stats(x3, xB, nb, out_rstd, inv_n, epsap):
            for k in range(KC):
                si = nxt("sq", 2)
                act(sqb[si][:, 0:nb], x3(k), AF.Square, [xB], [B("sqb%d" % si)])
                mm(pS[:, 0:nb], onesb, sqb[si][:, 0:nb], [Bcb, B("sqb%d" % si)], [B("pS")], start=(k == 0), stop=(k == KC - 1))
            act(out_rstd[:, 0:nb], pS[:, 0:nb], AF.Sqrt, [B("pS"), Beps], [B("rstd")], bias=epsap, scale=inv_n)
            recip(out_rstd[:, 0:nb], out_rstd[:, 0:nb], [B("rstd")], [B("rstd")])

        for l in range(L):
            last = (l == L - 1)
            xsrc = xT if l == 0 else XS
            xsrcB = B("xT") if l == 0 else B("XS")
            Bmod = B("modt")
            for k in range(KC):
                for piece in range(3):
                    dma(wst[:, :], w_ada[l, k * P:(k + 1) * P, piece * 1024:(piece + 1) * 1024], [], [B("wst")])
                    for jj in range(8):
                        j = piece * 8 + jj
                        mm(pM[:, j * 2:j * 2 + 2], wst[:, jj * P:(jj + 1) * P], sc[:, k * 2:k * 2 + 2], [B("wst"), B("sc")], [B("pM")])
                if k == 0:
                    cp(modt[:, :], pM[:, 0:48], [B("pM")], [Bmod])
                else:
                    tt(modt[:, :], modt[:, :], pM[:, 0:48], OP.add, [B("pM"), Bmod], [Bmod])
            for j in range(24):
                ts(modt[:, 2 * j:2 * j + 2], modt[:, 2 * j:2 * j + 2], V(("bada", l), j), None, OP.add, None, [Bmod, Bv], [Bmod])
            for k in range(KC):
                ts(gwt[:, 2 * k:2 * k + 2], modt[:, 2 * (8 + k):2 * (8 + k) + 2], 1.0, None, OP.add, None, [Bmod], [B("gwt")])
                ts(gwt[:, 2 * k:2 * k + 2], gwt[:, 2 * k:2 * k + 2], V(("normw", l), k), None, OP.mult, None, [B("gwt"), Bv], [B("gwt")])
            SH = lambda k, c: modt[:, 2 * k + c:2 * k + c + 1]
            GW = lambda k, c: gwt[:, 2 * k + c:2 * k + c + 1]
            GATE = lambda k, c: modt[:, 2 * (16 + k) + c:2 * (16 + k) + c + 1]

            if STOP < 1:
                break
            for k in range(KC):
                load_weight_bf16(winb, Bwin, w_in[l, k * P:(k + 1) * P, :], DIN, coloff=k * DIN)

            xv = xb[:, :].rearrange("p (k n) -> p k n", k=KC)
            hv = hb[:, :].rearrange("p (k n) -> p k n", k=KC)
            Bx, Bh = B("xb"), B("hb")
            for (t0, nb, c) in blocks:
                dma(xv[:, :, 0:nb], xsrc.rearrange("(k p) t -> p k t", p=P)[:, :, t0:t0 + nb], [xsrcB], [Bx])
                rms_stats(lambda k: xv[:, k, 0:nb], Bx, nb, rstd, 1.0 / DM, EPS6)
                for k in range(KC):
                    i = nxt("ev", 4)
                    tt(ev[i][:, 0:nb], xv[:, k, 0:nb], rstd[:, 0:nb], OP.mult, [Bx, B("rstd")], [B("ev%d" % i)])
                    act(hv[:, k, 0:nb], ev[i][:, 0:nb], AF.Identity, [B("ev%d" % i), Bmod, B("gwt")], [Bh], bias=SH(k, c), scale=GW(k, c))

                def proj(ct, w=P):
                    pi = nxt("pA", 2)
                    for k in range(KC):
                        mm(pA[pi][0:w, 0:nb], winb[:, k * DIN + ct * P: k * DIN + ct * P + w], hv[:, k, 0:nb], [Bwin, Bh], [B("pA%d" % pi)], start=(k == 0), stop=(k == KC - 1))
                    return pA[pi], B("pA%d" % pi)

                skip_conv = last and c == 0
                if not skip_conv:
                    for ct in range(4):
                        pg, pgB = proj(ct + 4)
                        i = nxt("ev", 4)
                        act(ev[i][:, 0:nb], pg[:, 0:nb], AF.Sigmoid, [pgB], [B("ev%d" % i)])
                        pa_, paB = proj(ct)
                        j = nxt("ev", 4)
                        tt(ev[j][:, 0:nb], pa_[:, 0:nb], ev[i][:, 0:nb], OP.mult, [paB, B("ev%d" % i)], [B("ev%d" % j)])
                        dma(Ud[ct * P:(ct + 1) * P, t0:t0 + nb], ev[j][:, 0:nb], [B("ev%d" % j)], [B("Ud")], eng="pool", key="st_ev%d" % j)
                    for ct in range(4):
                        pg, pgB = proj(8 + ct)
                        i = nxt("ev", 4)
                        act(ev[i][:, 0:nb], pg[:, 0:nb], AF.Silu, [pgB], [B("ev%d" % i)])
                        dma(CGd[ct * P:(ct + 1) * P, t0:t0 + nb], ev[i][:, 0:nb], [B("ev%d" % i)], [B("CGd")], eng="pool", key="st_ev%d" % i)
                for ct in range(12):
                    pg, pgB = proj(12 + ct)
                    i = nxt("ev", 4)
                    cp(ev[i][:, 0:nb], pg[:, 0:nb], [pgB], [B("ev%d" % i)], eng=("dve" if ct % 2 else "act"))
                    dma(QKVd[ct * P:(ct + 1) * P, t0:t0 + nb], ev[i][:, 0:nb], [B("ev%d" % i)], [B("QKVd")], eng="pool", key="st_ev%d" % i)
                for ct in range(4):
                    pg, pgB = proj(24 + ct)
                    i = nxt("ev", 4)
                    act(ev[i][:, 0:nb], pg[:, 0:nb], AF.Silu, [pgB], [B("ev%d" % i)])
                    dma(ZGd[ct * P:(ct + 1) * P, t0:t0 + nb], ev[i][:, 0:nb], [B("ev%d" % i)], [B("ZGd")], eng="pool", key="st_ev%d" % i)
                for tt_i in range(nb // P):
                    ti = t0 // P + tt_i
                    for k in range(KC):
                        mm(pM[:, 64:80], hv[:, k, tt_i * P:(tt_i + 1) * P], winb[:, k * DIN + 3584:k * DIN + 3600], [Bh, Bwin], [B("pM")], start=(k == 0), stop=(k == KC - 1))
                    cp(ABT[:, ti * 16:(ti + 1) * 16], pM[:, 64:80], [B("pM")], [B("ABT")])

            if STOP < 2:
                break
            abv = ABT[:, :].rearrange("p (t r) -> p t r", r=16)
            S3 = {n: SC[n][:, :].rearrange("p (r t) -> p r t", r=8) for n in SC}
            BS = B("SC")
            act(eal[:, :], V(("alog", l), 0, 8), AF.Exp, [Bv], [B("eal")])
            ts(eal[:, :], eal[:, :], -1.0, None, OP.mult, None, [B("eal")], [B("eal")])
            for r in range(8):
                ts(tmpc[:, :], abv[:, :, r], V(("dtb", l), r), None, OP.add, None, [B("ABT"), Bv], [B("tmpc")])
                act(tmpc2[:, :], tmpc[:, :], AF.Abs, [B("tmpc")], [B("tmpc2")])
                act(tmpc2[:, :], tmpc2[:, :], AF.Exp, [B("tmpc2")], [B("tmpc2")], scale=-1.0)
                act(tmpc2[:, :], tmpc2[:, :], AF.Ln, [B("tmpc2"), Beps], [B("tmpc2")], bias=ONE1)
                stt(tmpc[:, :], tmpc[:, :], 0.0, tmpc2[:, :], OP.max, OP.add, [B("tmpc"), B("tmpc2")], [B("tmpc")])
                ts(S3["G"][:, r, :], tmpc[:, :], eal[:, r:r + 1], None, OP.mult, None, [B("tmpc"), B("eal")], [BS])
                act(S3["BE"][:, r, :], abv[:, :, 8 + r], AF.Sigmoid, [B("ABT")], [BS])
            for dr in range(2):
                sl = slice(dr * 4 * NT, (dr + 1) * 4 * NT)
                mm(pS[:, 0:4 * NT], tri[dr], SC["G"][:, sl], [Bc, BS], [B("pS")])
                cp(SC["GC"][:, sl], pS[:, 0:4 * NT], [B("pS")], [BS])
                mm(pS[:, 0:4 * NT], ones, SC["G"][:, sl], [Bc, BS], [B("pS")])
                cp(SC["GT"][:, sl], pS[:, 0:4 * NT], [B("pS")], [BS])
            act(SC["EG"][:, :], SC["GC"][:, :], AF.Exp, [BS], [BS])
            act(SC["GL"][:, :], SC["GT"][:, :], AF.Exp, [BS], [BS])
            tt(SC["EKT"][:, :], SC["GT"][:, :], SC["GC"][:, :], OP.subtract, [BS], [BS])
            act(SC["EKT"][:, :], SC["EKT"][:, :], AF.Exp, [BS], [BS])
            ts(SC["NBE"][:, :], SC["BE"][:, :], -1.0, None, OP.mult, None, [BS], [BS])
            tt(SC["BKG"][:, :], SC["BE"][:, :], SC["EG"][:, :], OP.mult, [BS], [BS])
            SCL = lambda n, dr, hd, ti: SC[n][:, (dr * 4 + hd) * NT + ti:(dr * 4 + hd) * NT + ti + 1]

            if STOP < 3:
                break
            def dwconv(src, srcBs, dst, dstBs, wcol, ntap, first_bias, mode, do_ctx=True):
                half = (ntap - 1) // 2
                segs = []
                if do_ctx:
                    segs.append(("ctx", "dve"))
                segs.append(("lat", "dve"))
                for seg, eng in segs:
                    sB, dB = srcBs[seg], dstBs[seg]
                    lo, hi = (0, TC) if seg == "ctx" else (TC, T)
                    if first_bias is None:
                        ts(dst[:, lo:hi], src[:, lo:hi], wcol(half), None, OP.mult, None, [sB, Bv], [dB], eng=eng)
                    else:
                        ts(dst[:, lo:hi], src[:, lo:hi], wcol(half), first_bias, OP.mult, OP.add, [sB, Bv], [dB], eng=eng)
                    m = "seq" if seg == "ctx" else mode
                    for k in range(ntap):
                        s = k - half
                        if s == 0:
                            continue
                        if m == "seq":
                            n = hi - lo
                            a0, a1 = max(0, -s), n - max(0, s)
                            if a1 <= a0:
                                continue
                            o_ap = dst[:, lo + a0:lo + a1]
                            i_ap = src[:, lo + a0 + s:lo + a1 + s]
                        elif m == "h":
                            dv_ = dst[:, lo:hi].rearrange("p (r c) -> p r c", c=64)
                            sv_ = src[:, lo:hi].rearrange("p (r c) -> p r c", c=64)
                            a0, a1 = max(0, -s), 64 - max(0, s)
                            o_ap = dv_[:, :, a0:a1]
                            i_ap = sv_[:, :, a0 + s:a1 + s]
                        else:
                            r0, r1 = max(0, -s), R - max(0, s)
                            if r1 <= r0:
                                continue
                            o_ap = dst[:, lo + 64 * r0:lo + 64 * r1]
                            i_ap = src[:, lo + 64 * (r0 + s):lo + 64 * (r1 + s)]
                        stt(o_ap, i_ap, wcol(k), o_ap, OP.mult, OP.add, [sB, dB, Bv], [dB], eng=eng)

            BAs = {"ctx": BA, "lat": BA}
            BBs = {"ctx": BB, "lat": BB}
            allA = [BA]
            allB = [BB]
            for ct in range(4):
                dma(BIGA[:, :], Ud[ct * P:(ct + 1) * P, :], [B("Ud")], allA)
                dwconv(BIGA, BAs, BIGB, BBs, lambda k, ct=ct: V(("convw", l), ct * 31 + k), 31,
                       V(("convb", l), ct), "h" if ct < 2 else "v", do_ctx=not last)
                dma(CVd[ct * P:(ct + 1) * P, :], BIGB[:, :], allB, [B("CVd")], eng="pool", key="st_BIGB")

            if STOP < 4:
                break
            psum_fence()
            Ov = BIGB[:, :].rearrange("p (t d) -> p t d", d=P)
            fwd_order = list(range(NT))
            bwd_order = [1, 0] + list(range(NT - 1, 1, -1))
            for hd in range(4):
                for m_i, dstT in ((0, qT), (1, kT), (2, VTt)):
                    if m_i >= int(os.environ.get('KSUB', '3')):
                        continue
                    row0 = m_i * 512 + hd * P
                    dma(BIGA[:, :], QKVd[row0:row0 + P, :], [B("QKVd")], allA)
                    dwconv(BIGA, BAs, BIGB, BBs, lambda k, m_i=m_i: V(("scw", l), (m_i * 4 + hd) * 5 + k), 5, None, "seq")
                    act(BIGB[:, :], BIGB[:, :], AF.Silu, [BB], allB)
                    if m_i < 2:
                        for t0 in range(0, T, NB):
                            si = nxt("sq", 2)
                            act(sqb[si][:, :], BIGB[:, t0:t0 + NB], AF.Square, [BB], [B("sqb%d" % si)])
                            mm(pS[:, 0:NB], onesb, sqb[si][:, :], [Bcb, B("sqb%d" % si)], [B("pS")])
                            act(rstd[:, :], pS[:, 0:NB], AF.Sqrt, [B("pS"), Beps], [B("rstd")], bias=EPS6, scale=1.0)
                            recip(rstd[:, :], rstd[:, :], [B("rstd")], [B("rstd")])
                            if m_i == 0:
                                stt(dstT[:, t0:t0 + NB], BIGB[:, t0:t0 + NB], float(P) ** -0.5, rstd[:, :], OP.mult, OP.mult, [BB, B("rstd")], [Bwin])
                            else:
                                tt(dstT[:, t0:t0 + NB], BIGB[:, t0:t0 + NB], rstd[:, :], OP.mult, [BB, B("rstd")], [Bwin])
                    else:
                        for ti in range(NT):
                            ps_, psB = pM[:, 384:512], B("pM")
                            si = nxt("sq", 2)
                            cp(sqb[si][:, 0:P], BIGB[:, ti * P:(ti + 1) * P], [BB], [B("sqb%d" % si)], eng="act")
                            mm(ps_, sqb[si][:, 0:P], identb, [B("sqb%d" % si), Bcb], [psB])
                            cp(VTt[:, ti * P:(ti + 1) * P], ps_, [psB], [Bwin], eng="dve")

                if STOP < 5:
                    continue
                def big_fence():
                    ws = [BA, B("fz")] + [B("%s_%d" % (n, sl)) for sl in range(2, 2 * NSTEP) for n in DTF + DTB]
                    A.op("pool", lambda e: e.memset(fz[:, :], 0.0), writes=ws)

                big_fence()
                for dr in range(2):
                    A.op("pool", lambda e, dr=dr: e.memset(ST[("Sf", dr)], 0.0), writes=[B("Sf%d" % dr)])
                    A.op("pool", lambda e, dr=dr: e.memset(ST[("Sb", dr)], 0.0), writes=[B("Sb%d" % dr)])
                seen = set()

                def setup_ops(sl, dr, ti):
                    d = lambda n: DT[(n, sl)]
                    bb = lambda n: B("%s_%d" % (n, sl))
                    tsl = slice(ti * P, (ti + 1) * P)
                    pKK, pKKB = pd(0 + 6 * sl)
                    pQK, pQKB = pd(1 + 6 * sl)
                    pZ, pZB = pd(2 + 6 * sl)
                    p3, p3B = pd(3 + 6 * sl)
                    p4, p4B = pd(4 + 6 * sl)
                    p5, p5B = pd(5 + 6 * sl)
                    gc = SCL("GC", dr, hd, ti)
                    ops = []
                    ops.append(lambda: (mm(pKK, kT[:, tsl], kT[:, tsl], [Bwin], [pKKB]),
                                        ts(d("Gd"), ident, gc, None, OP.mult, None, [Bc, BS], [bb("Gd")])))
                    ops.append(lambda: (mm(pZ, ones, d("Gd"), [Bc, bb("Gd")], [pZB], start=True, stop=False),
                                        mm(pZ, ident, pen[dr], [Bc], [pZB], start=False, stop=True)))
                    ops.append(lambda: (ts(d("D"), pZ, gc, None, OP.subtract, None, [pZB, BS], [bb("D")]),
                                        mm(pQK, qT[:, tsl], kT[:, tsl], [Bwin], [pQKB])))
                    ops.append(lambda: act(d("D"), d("D"), AF.Exp, [bb("D")], [bb("D")], scale=-1.0))
                    ops.append(lambda: tt(d("DMs"), d("D"), msk[dr], OP.mult, [bb("D"), Bc], [bb("DMs")], eng="pool"))
                    ops.append(lambda: (stt(d("N0"), pKK, SCL("NBE", dr, hd, ti), d("DMs"), OP.mult, OP.mult, [pKKB, BS, bb("DMs")], [bb("N0")]),
                                        tt(d("attn"), pQK, d("D"), OP.mult, [pQKB, bb("D")], [bb("attn")])))
                    ops.append(lambda: (mm(p4, d("N0"), ident, [bb("N0"), Bc], [p4B]),
                                        mm(p3, d("attn"), identb, [bb("attn"), Bcb], [p3B]),
                                        mm(p5, kT[:, tsl], identb, [Bwin, Bcb], [p5B])))
                    ops.append(lambda: (cp(d("U0"), p4, [p4B], [bb("U0")], eng="act"),
                                        cp(d("attnT"), p3, [p3B], [bb("attnT")], eng="act"),
                                        ts(d("kbg"), p5, SCL("BKG", dr, hd, ti), None, OP.mult, None, [p5B, BS], [bb("kbg")])))
                    ops.append(lambda: (tt(d("Y"), d("U0"), ident, OP.add, [bb("U0"), Bc], [bb("Y")], eng="pool"),
                                        ts(d("kt"), p5, SCL("EKT", dr, hd, ti), None, OP.mult, None, [p5B, BS], [bb("kt")]),
                                        act(d("vb"), VTt[:, tsl], AF.Identity, [Bwin, BS], [bb("vb")], scale=SCL("BE", dr, hd, ti))))
                    return ops

                assert NT % NSTEP == 0
                for pair in range(NT // NSTEP):
                    chains = []
                    for j in range(NSTEP):
                        step = NSTEP * pair + j
                        chains.append((2 * j + 0, 0, fwd_order[step]))
                        chains.append((2 * j + 1, 1, bwd_order[step]))
                    allops = [setup_ops(sl, dr, ti) for sl, dr, ti in chains]
                    for i in range(len(allops[0])):
                        for ops in allops:
                            ops[i]()
                    cur = {sl: ("N0", "U0") for sl, _, _ in chains}
                    for lvl in range(1, 7):
                        info = []
                        for sl, dr, ti in chains:
                            d = lambda n, sl=sl: DT[(n, sl)]
                            bb = lambda n, sl=sl: B("%s_%d" % (n, sl))
                            nN, nU = cur[sl]
                            oN, oU = ("N1", "U1") if nN == "N0" else ("N0", "U0")
                            pa_, paB = pd(3 + 6 * sl)
                            pb_, pbB = pd(4 + 6 * sl)
                            mm(pb_, d(nU), d(nN), [bb(nU), bb(nN)], [pbB])
                            if lvl < 6:
                                mm(pa_, d(nN), d(nU), [bb(nU), bb(nN)], [paB])
                            info.append((sl, d, bb, oN, oU, pa_, paB, pb_, pbB))
                            cur[sl] = (oN, oU)
                        for sl, d, bb, oN, oU, pa_, paB, pb_, pbB in info:
                            cp(d(oN), pb_, [pbB], [bb(oN)], eng=("act" if sl % 2 else "dve"))
                            if lvl < 6:
                                cp(d(oU), pa_, [paB], [bb(oU)], eng=("dve" if sl % 2 else "act"))
                        for sl, d, bb, oN, oU, pa_, paB, pb_, pbB in info:
                            pc_, pcB = pd(5 + 6 * sl)
                            mm(pc_, d(oN), d("Y"), [bb(oN), bb("Y")], [pcB])
                        for sl, d, bb, oN, oU, pa_, paB, pb_, pbB in info:
                            pc_, pcB = pd(5 + 6 * sl)
                            tt(d("Y"), d("Y"), pc_, OP.add, [pcB, bb("Y")], [bb("Y")])
                    for sl, dr, ti in chains:
                        d = lambda n, sl=sl: DT[(n, sl)]
                        bb = lambda n, sl=sl: B("%s_%d" % (n, sl))
                        cp(d("TTb"), d("Y"), [bb("Y")], [bb("TTb")], eng="act")
                    for sl, dr, ti in chains:
                        d = lambda n, sl=sl: DT[(n, sl)]
                        bb = lambda n, sl=sl: B("%s_%d" % (n, sl))
                        pa_, paB = pd(3 + 6 * sl)
                        pb_, pbB = pd(4 + 6 * sl)
                        mm(pa_, d("TTb"), d("vb"), [bb("TTb"), bb("vb")], [paB])
                        mm(pb_, d("kbg"), d("TTb"), [bb("kbg"), bb("TTb")], [pbB])
                    for sl, dr, ti in chains:
                        d = lambda n, sl=sl: DT[(n, sl)]
                        bb = lambda n, sl=sl: B("%s_%d" % (n, sl))
                        pa_, paB = pd(3 + 6 * sl)
                        pb_, pbB = pd(4 + 6 * sl)
                        cp(d("u"), pa_, [paB], [bb("u")], eng="act")
                        cp(d("wT"), pb_, [pbB], [bb("wT")], eng="dve")
                    for j in range(NSTEP):
                        cj = [c for c in chains if c[0] // 2 == j]
                        for sl, dr, ti in cj:
                            d = lambda n, sl=sl: DT[(n, sl)]
                            bb = lambda n, sl=sl: B("%s_%d" % (n, sl))
                            Sf, Sbb = ST[("Sf", dr)], ST[("Sb", dr)]
                            SfB, SbB = B("Sf%d" % dr), B("Sb%d" % dr)
                            tsl = slice(ti * P, (ti + 1) * P)
                            pc_, pcB = pd(3 + 6 * sl)
                            pe_, peB = pd(0 + 6 * sl)
                            pf_, pfB = pd(1 + 6 * sl)
                            pg_, pgB = pd(2 + 6 * sl)
                            mm(pc_, d("wT"), Sbb, [bb("wT"), SbB], [pcB])
                            mm(pe_, qT[:, tsl], Sbb, [Bwin, SbB], [peB])
                            tt(d("vn"), d("u"), pc_, OP.subtract, [bb("u"), pcB], [bb("vn")])
                            mm(pg_, d("kt"), d("vn"), [bb("kt"), bb("vn")], [pgB])
                            mm(pf_, d("attnT"), d("vn"), [bb("attnT"), bb("vn")], [pfB])
                            stt(Sf, Sf, SCL("GL", dr, hd, ti), pg_, OP.mult, OP.add, [SfB, BS, pgB], [SfB])
                            cp(Sbb, Sf, [SfB], [SbB], eng="act")
                            ts(d("t"), pe_, SCL("EG", dr, hd, ti), None, OP.mult, None, [peB, BS], [bb("t")])
                            if ti in seen:
                                tt(Ov[:, ti, :], Ov[:, ti, :], d("t"), OP.add, [bb("t"), BB], [BB])
                                tt(Ov[:, ti, :], Ov[:, ti, :], pf_, OP.add, [pfB, BB], [BB])
                            else:
                                tt(Ov[:, ti, :], d("t"), pf_, OP.add, [bb("t"), pfB], [BB])
                                seen.add(ti)
                big_fence()

                if int(os.environ.get('KSC', '3')) < 3 or int(os.environ.get('KOUT', '1')) < 1:
                    continue
                do_out_ctx = not last
                act(BIGA[:, :], BIGB[:, :], AF.Square, [BB], allA)
                A.op("dve", lambda e: e.tensor_reduce(out=orstd[:, :], in_=BIGA[:, :].rearrange("p (t d) -> p t d", d=P), axis=AX.X, op=OP.add), reads=[BA], writes=[B("orstd")])
                act(orstd[:, :], orstd[:, :], AF.Sqrt, [B("orstd"), Beps], [B("orstd")], bias=EPS6, scale=1.0 / P)
                recip(orstd[:, :], orstd[:, :], [B("orstd")], [B("orstd")])
                dma(BIGA[:, :], ZGd[hd * P:(hd + 1) * P, :], [B("ZGd")], allA)
                for ti in range(0 if do_out_ctx else 2, NT):
                    stt(ont[:, :], Ov[:, ti, :], orstd[:, ti:ti + 1], V(("dnw", l), 0, P), OP.mult, OP.mult, [BB, B("orstd"), Bv], [B("ont")])
                    ps_, psB = pM[:, 384:512], B("pM")
                    mm(ps_, ont[:, :], ident, [B("ont"), Bc], [psB])
                    tt(ydt[:, :], ps_, BIGA[:, ti * P:(ti + 1) * P], OP.mult, [psB, BA], [B("ydt")])
                    dma(YDd[hd * P:(hd + 1) * P, ti * P:(ti + 1) * P], ydt[:, :], [B("ydt")], [B("YDd")], eng="pool", key="st_ydt")

            if STOP < 6:
                break
            psum_fence()
            for k in range(KC):
                load_weight_bf16(woutb, BA, w_out[l, k * P:(k + 1) * P, :], DM, coloff=k * DM)
            cvv = cvb[:, :].rearrange("p (k n) -> p k n", k=4)
            cgv = cgb[:, :].rearrange("p (k n) -> p k n", k=4)
            for (t0, nb, c) in blocks:
                if last and c == 0:
                    continue
                dma(cvv[:, :, 0:nb], CVd.rearrange("(k p) t -> p k t", p=P)[:, :, t0:t0 + nb], [B("CVd")], [B("cvb")])
                dma(cgv[:, :, 0:nb], CGd.rearrange("(k p) t -> p k t", p=P)[:, :, t0:t0 + nb], [B("CGd")], [B("cgb")])
                dma(hv[:, 4:8, 0:nb], YDd.rearrange("(k p) t -> p k t", p=P)[:, :, t0:t0 + nb], [B("YDd")], [Bh])
                dma(xv[:, :, 0:nb], xsrc.rearrange("(k p) t -> p k t", p=P)[:, :, t0:t0 + nb], [xsrcB], [Bx])
                for ct in range(4):
                    mm(pS[:, 0:nb], ones, cvv[:, ct, 0:nb], [Bc, B("cvb")], [B("pS")], start=(ct == 0), stop=(ct == 3))
                act(mean[:, 0:nb], pS[:, 0:nb], AF.Copy, [B("pS")], [B("mean")], scale=1.0 / 512)
                for ct in range(4):
                    i = nxt("ev", 4)
                    act(ev[i][:, 0:nb], cvv[:, ct, 0:nb], AF.Square, [B("cvb")], [B("ev%d" % i)])
                    mm(pM[:, 0:nb], ones, ev[i][:, 0:nb], [Bc, B("ev%d" % i)], [B("pM")], start=(ct == 0), stop=(ct == 3))
                i = nxt("ev", 4)
                tt(ev[i][:, 0:nb], mean[:, 0:nb], mean[:, 0:nb], OP.mult, [B("mean")], [B("ev%d" % i)])
                stt(rstd[:, 0:nb], pM[:, 0:nb], 1.0 / 512, ev[i][:, 0:nb], OP.mult, OP.subtract, [B("pM"), B("ev%d" % i)], [B("rstd")])
                act(rstd[:, 0:nb], rstd[:, 0:nb], AF.Sqrt, [B("rstd"), Beps], [B("rstd")], bias=EPS5, scale=1.0)
                recip(rstd[:, 0:nb], rstd[:, 0:nb], [B("rstd")], [B("rstd")])
                for ct in range(4):
                    i = nxt("ev", 4)
                    tt(ev[i][:, 0:nb], cvv[:, ct, 0:nb], mean[:, 0:nb], OP.subtract, [B("cvb"), B("mean")], [B("ev%d" % i)])
                    tt(ev[i][:, 0:nb], ev[i][:, 0:nb], rstd[:, 0:nb], OP.mult, [B("ev%d" % i), B("rstd")], [B("ev%d" % i)])
                    act(ev[i][:, 0:nb], ev[i][:, 0:nb], AF.Silu, [B("ev%d" % i), Bv], [B("ev%d" % i)], bias=V(("lnb", l), ct), scale=V(("lnw", l), ct))
                    tt(hv[:, ct, 0:nb], ev[i][:, 0:nb], cgv[:, ct, 0:nb], OP.mult, [B("ev%d" % i), B("cgb")], [Bh])
                for fc in range(KC):
                    pi = nxt("pA", 2)
                    for k in range(KC):
                        mm(pA[pi][:, 0:nb], woutb[:, k * DM + fc * P:k * DM + (fc + 1) * P], hv[:, k, 0:nb], [BA, Bh], [B("pA%d" % pi)], start=(k == 0), stop=(k == KC - 1))
                    stt(xv[:, fc, 0:nb], pA[pi][:, 0:nb], GATE(fc, c), xv[:, fc, 0:nb], OP.mult, OP.add, [B("pA%d" % pi), Bmod, Bx], [Bx])
                if not last:
                    dma(XS.rearrange("(k p) t -> p k t", p=P)[:, :, t0:t0 + nb], xv[:, :, 0:nb], [Bx], [B("XS")], eng="pool", key="st_xb")
                else:
                    rms_stats(lambda k: xv[:, k, 0:nb], Bx, nb, rstd, 1.0 / DM, EPS6)
                    for k in range(KC):
                        stt(xv[:, k, 0:nb], xv[:, k, 0:nb], V("fnw", k), rstd[:, 0:nb], OP.mult, OP.mult, [Bx, Bv, B("rstd")], [Bx])
                    dma(outT.rearrange("(k p) t -> p k t", p=P)[:, :, t0 - TC:t0 - TC + nb], xv[:, :, 0:nb], [Bx], [B("outT")], eng="pool", key="st_xb")

        A.final_waits = ["st_xb"]
        A.emit(nc)
    return nc


def make_consts():
    i = np.arange(P)
    c = np.zeros((P, 8, P), np.float32)
    c[:, 0] = np.eye(P)
    c[:, 1] = 1.0
    c[:, 2] = (i[:, None] <= i[None, :])
    c[:, 3] = (i[:, None] >= i[None, :])
    c[:, 4] = np.where(i[:, None] >= i[None, :], 0.0, BIG)
    c[:, 5] = np.where(i[:, None] <= i[None, :], 0.0, BIG)
    c[:, 6] = (i[:, None] > i[None, :])
    c[:, 7] = (i[:, None] < i[None, :])
    return c.reshape(P, 8 * P)


def make_vec(L, b, c, c_ctx, norm_w, b_ada, conv_w, conv_b, conv_ln_w, conv_ln_b, short_conv_w, a_log,
             dt_bias, dn_norm_w, final_norm_w):
    voff, NV = vec_layout(L)
    v = np.zeros((P, NV), np.float32)
    pk = lambda a, n: np.asarray(a, np.float32).reshape(n, P).T
    for l in range(L):
        v[:, voff[("normw", l)]:voff[("normw", l)] + 8] = pk(norm_w[l], 8)
        v[:, voff[("bada", l)]:voff[("bada", l)] + 24] = pk(b_ada[l], 24)
        cw = np.asarray(conv_w[l], np.float32)
        v[:, voff[("convw", l)]:voff[("convw", l)] + 124] = cw.reshape(31, 4, P).transpose(2, 1, 0).reshape(P, 124)
        v[:, voff[("convb", l)]:voff[("convb", l)] + 4] = pk(conv_b[l], 4)
        v[:, voff[("lnw", l)]:voff[("lnw", l)] + 4] = pk(conv_ln_w[l], 4)
        v[:, voff[("lnb", l)]:voff[("lnb", l)] + 4] = pk(conv_ln_b[l], 4)
        sw = np.asarray(short_conv_w[l], np.float32)
        v[:, voff[("scw", l)]:voff[("scw", l)] + 60] = sw.reshape(5, 12, P).transpose(2, 1, 0).reshape(P, 60)
        v[:, voff[("alog", l)]:voff[("alog", l)] + 8] = np.asarray(a_log[l], np.float32).reshape(1, 8)
        v[:, voff[("dtb", l)]:voff[("dtb", l)] + 8] = np.asarray(dt_bias[l], np.float32).reshape(1, 8)
        v[:, voff[("dnw", l)]:voff[("dnw", l)] + P] = np.asarray(dn_norm_w[l], np.float32).reshape(1, P)
    v[:, voff["fnw"]:voff["fnw"] + 8] = pk(final_norm_w, 8)
    cond = np.stack([pk(c_ctx, 8), pk(c[b], 8)], axis=-1).reshape(P, 16)
    v[:, voff["cond"]:voff["cond"] + 16] = cond
    return v


_prog_cache = {}


def kernel(x, c, ctx, c_ctx, norm_w, w_ada, b_ada, w_in, conv_w, conv_b, conv_ln_w, conv_ln_b,
           short_conv_w, a_log, dt_bias, dn_norm_w, w_out, final_norm_w):
    x = np.asarray(x, np.float32)
    ctx = np.asarray(ctx, np.float32)
    Bsz, SEQ, _ = x.shape
    L = np.asarray(norm_w).shape[0]
    R = SEQ // 64
    key = (R, L)
    if key not in _prog_cache:
        _prog_cache[key] = build_program(R, L)
    nc = _prog_cache[key]
    cstv = make_consts()
    w_ada_f = np.ascontiguousarray(np.asarray(w_ada, np.float32))
    w_in_f = np.ascontiguousarray(np.asarray(w_in, np.float32))
    w_out_f = np.ascontiguousarray(np.asarray(w_out, np.float32))
    in_maps = []
    for b in range(Bsz):
        xTb = np.ascontiguousarray(np.concatenate([ctx[b], x[b]], axis=0).T)
        vecb = make_vec(L, b, np.asarray(c, np.float32), np.asarray(c_ctx, np.float32), norm_w, b_ada, conv_w, conv_b,
                        conv_ln_w, conv_ln_b, short_conv_w, a_log, dt_bias, dn_norm_w, final_norm_w)
        in_maps.append({"xT": xTb, "vec_in": vecb, "cst_in": cstv, "w_ada": w_ada_f, "w_in": w_in_f, "w_out": w_out_f})
    res = run_bass_kernel_spmd(nc, in_maps, core_ids=list(range(Bsz)))
    out = np.stack([np.ascontiguousarray(r["outT"].T) for r in res.results], axis=0)
    return out.astype(np.float32)
```

```python
import os
import numpy as np
from contextlib import ExitStack
import concourse.bass as bass
import concourse.mybir as mybir
from concourse.bass_utils import run_bass_kernel_spmd

F32 = mybir.dt.float32
BF16 = mybir.dt.bfloat16
AF = mybir.ActivationFunctionType
OP = mybir.AluOpType
AX = mybir.AxisListType

P = 128
DM = 1024
KC = 8
DIN = 3600
TC = 256
NB = 256
BIG = 1.0e4
ENGS = ("pe", "act", "dve", "pool", "sp")
STOP = int(os.environ.get("KSTOP", "9"))


class Buf:
    __slots__ = ("name", "last_w", "readers", "excl")

    def __init__(self, name, excl=False):
        self.name, self.last_w, self.readers, self.excl = name, None, [], excl


class Node:
    __slots__ = ("eng", "fn", "deps", "dma", "key", "signal", "cnt", "ep")

    def __init__(self, eng, fn, deps, dma, key):
        self.eng, self.fn, self.deps, self.dma, self.key = eng, fn, deps, dma, key
        self.signal, self.cnt, self.ep = dma, 0, 0


class Asm:
    def __init__(self):
        self.ins = {e: [] for e in ENGS}
        self.dma_keys = {}
        self.final_waits = []

    def op(self, eng, fn, reads=(), writes=(), dma=False, key=None):
        ex = [b for b in reads if b.excl]
        if ex:
            reads = [b for b in reads if not b.excl]
            writes = list(writes) + [b for b in ex if b not in writes]
        deps = []
        for b in reads:
            if b.last_w is not None:
                deps.append(b.last_w)
        for b in writes:
            if b.last_w is not None:
                deps.append(b.last_w)
            deps.extend(b.readers)
        if dma:
            if key is None:
                key = writes[0].name
            self.dma_keys.setdefault(key, 0)
        node = Node(eng, fn, deps, dma, key)
        for b in reads:
            b.readers.append(node)
        for b in writes:
            b.last_w = node
            b.readers = []
        self.ins[eng].append(node)
        return node

    def emit(self, nc):
        for e in ENGS:
            for n in self.ins[e]:
                for d in n.deps:
                    if d is n or d.dma:
                        continue
                    if d.eng == "pe" and n.eng == "pe" and not n.dma:
                        continue
                    d.signal = True
        keycnt = {k: 0 for k in self.dma_keys}
        EPOCH = int(os.environ.get("KEPOCH", "16000"))
        nep = {e: 1 for e in ENGS}
        for e in ENGS:
            c = 0
            ep = 0
            for n in self.ins[e]:
                if n.dma:
                    keycnt[n.key] += 16
                    n.cnt = keycnt[n.key]
                elif n.signal:
                    if c >= EPOCH:
                        c = 0
                        ep += 1
                    c += 1
                    n.cnt, n.ep = c, ep
            nep[e] = ep + 1
        assert max(keycnt.values() or [0]) < 32000
        with ExitStack() as st:
            esem = {(e, i): st.enter_context(nc.semaphore("sem_%s%d" % (e, i))) for e in ENGS if e != "sp" for i in range(nep[e])}
            ksem = {k: st.enter_context(nc.semaphore("dk_%d" % i)) for i, k in enumerate(self.dma_keys)}
            block = st.enter_context(nc.Block())
            lastk = {}
            for e in ENGS:
                for n in self.ins[e]:
                    if n.dma and n.key in self.final_waits:
                        lastk[n.key] = max(lastk.get(n.key, 0), n.cnt)

            def run(e, eng):
                waited = {}
                for n in self.ins[e]:
                    for d in n.deps:
                        if d is n:
                            continue
                        if d.dma:
                            s, v, kk = ksem[d.key], d.cnt, ("k", d.key)
                        else:
                            if d.eng == "pe" and e == "pe" and not n.dma:
                                continue
                            s, v, kk = esem[(d.eng, d.ep)], d.cnt, ("e", d.eng, d.ep)
                        if waited.get(kk, 0) >= v:
                            continue
                        waited[kk] = v
                        eng.wait_ge(s, v)
                    ins = n.fn(eng)
                    if n.dma:
                        ins.then_inc(ksem[n.key], 16)
                    elif n.signal:
                        ins.then_inc(esem[(e, n.ep)], 1)
                if e == "sp":
                    for k, v in lastk.items():
                        eng.wait_ge(ksem[k], v)

            block.tensor(lambda eng: run("pe", eng))
            block.scalar(lambda eng: run("act", eng))
            block.vector(lambda eng: run("dve", eng))
            block.gpsimd(lambda eng: run("pool", eng))
            block.sync(lambda eng: run("sp", eng))


def vec_layout(L):
    off = {}
    o = 0
    per = [("normw", 8), ("bada", 24), ("convw", 124), ("convb", 4), ("lnw", 4), ("lnb", 4),
           ("scw", 60), ("alog", 8), ("dtb", 8), ("dnw", 128)]
    for l in range(L):
        for nm, n in per:
            off[(nm, l)] = o
            o += n
    off["fnw"] = o
    o += 8
    off["cond"] = o
    o += 16
    return off, o


def build_program(R, L):
    TL = 64 * R
    T = TC + TL
    NT = T // P
    assert TL % NB == 0
    blocks = [(0, NB, 0)] + [(TC + i * NB, NB, 1) for i in range(TL // NB)]
    voff, NV = vec_layout(L)

    nc = bass.Bass("TRN2", target_bir_lowering=False)
    d_in = lambda n, s: nc.dram_tensor(n, s, F32, kind="ExternalInput").ap()
    xT = d_in("xT", [DM, T])
    vec_d = d_in("vec_in", [P, NV])
    cst_d = d_in("cst_in", [P, 8 * P])
    w_ada = d_in("w_ada", [L, DM, 3 * DM])
    w_in = d_in("w_in", [L, DM, DIN])
    w_out = d_in("w_out", [L, DM, DM])
    outT = nc.dram_tensor("outT", [DM, TL], F32, kind="ExternalOutput").ap()
    XS = nc.dram_tensor("XS", [DM, T], F32).ap()
    Ud = nc.dram_tensor("Ud", [512, T], F32).ap()
    CGd = nc.dram_tensor("CGd", [512, T], F32).ap()
    QKVd = nc.dram_tensor("QKVd", [1536, T], F32).ap()
    ZGd = nc.dram_tensor("ZGd", [512, T], F32).ap()
    CVd = nc.dram_tensor("CVd", [512, T], F32).ap()
    YDd = nc.dram_tensor("YDd", [512, T], BF16).ap()

    A = Asm()
    bufs = {}

    def B(name):
        if name not in bufs:
            bufs[name] = Buf(name)
        return bufs[name]

    with ExitStack() as st:
        sb = lambda n, s, d=F32: st.enter_context(nc.sbuf_tensor(n, s, d))
        pst = lambda n, s: st.enter_context(nc.psum_tensor(n, s, F32))
        winb = sb("winb", [P, KC * DIN], BF16)
        BIGA = sb("BIGA", [P, T])
        BIGB = sb("BIGB", [P, T])
        wst = sb("wst", [P, 1024])
        xb = sb("xb", [P, KC * NB])
        hb = sb("hb", [P, KC * NB], BF16)
        sqb = [sb("sqb%d" % i, [P, NB], BF16) for i in range(2)]
        ev = [sb("ev%d" % i, [P, NB]) for i in range(4)]
        cvb = sb("cvb", [P, 4 * NB])
        cgb = sb("cgb", [P, 4 * NB])
        rstd = sb("rstd", [P, NB])
        mean = sb("mean", [P, NB])
        vec = sb("vec", [P, NV])
        cst = sb("cst", [P, 8 * P])
        cstb = sb("cstb", [P, 2 * P], BF16)
        epsc = sb("epsc", [P, 4])
        sc = sb("sc", [P, 16])
        modt = sb("modt", [P, 48])
        gwt = sb("gwt", [P, 16])
        ABT = sb("ABT", [P, NT * 16])
        SC = {n: sb("S_" + n, [P, 8 * NT]) for n in ("G", "GC", "GT", "EG", "GL", "BE", "NBE")}
        SC["EKT"] = SC["GT"]
        SC["BKG"] = SC["G"]
        eal = sb("eal", [P, 8])
        tmpc = sb("tmpc", [P, NT])
        tmpc2 = sb("tmpc2", [P, NT])
        orstd = sb("orstd", [P, NT])
        DTF = ("Gd", "D", "DMs", "N0", "U0", "N1", "U1", "Y", "u", "t")
        DTB = ("attn", "attnT", "TTb", "wT", "kbg", "kt", "vb", "vn")
        DT = {}
        for sl in range(2):
            for n in DTF:
                DT[(n, sl)] = sb("d_%s%d" % (n, sl), [P, P])[:, :]
            for n in DTB:
                DT[(n, sl)] = sb("d_%s%d" % (n, sl), [P, P], BF16)[:, :]
        off = 0
        need = 2 * (len(DTF) * P + len(DTB) * P // 2)
        carve = BIGA if need <= T else sb("dtx", [P, need])
        for sl in (2, 3):
            for n in DTF:
                DT[(n, sl)] = carve[:, off:off + P]
                off += P
            for n in DTB:
                DT[(n, sl)] = carve[:, off:off + P // 2].bitcast(BF16)
                off += P // 2
        ST = {}
        for dr in range(2):
            ST[("Sf", dr)] = sb("d_Sf%d" % dr, [P, P])[:, :]
            ST[("Sb", dr)] = sb("d_Sb%d" % dr, [P, P], BF16)[:, :]
        fz = sb("fz", [P, 1])
        ont = sb("ont", [P, P])
        ydt = sb("ydt", [P, P], BF16)

        bank = [pst("bank%d" % i, [P, 512]) for i in range(8)]
        for i in range(8):
            bufs["bank%d" % i] = Buf("bank%d" % i, excl=True)
        for nm, i in (("pA0", 0), ("pA1", 1), ("pS", 2), ("pM", 3)):
            bufs[nm] = bufs["bank%d" % i]
        pA = [bank[0], bank[1]]
        pS, pM = bank[2], bank[3]
        def pd(i):
            ch_, sl_ = i // 6, i % 6
            bi = 2 * ch_ + (sl_ % 2)
            c0 = (sl_ // 2) * P
            return bank[bi][:, c0:c0 + P], bufs["bank%d" % bi]

        def psum_fence():
            pass

        ident = cst[:, 0:P]
        ones = cst[:, P:2 * P]
        tri = [cst[:, 2 * P:3 * P], cst[:, 3 * P:4 * P]]
        pen = [cst[:, 4 * P:5 * P], cst[:, 5 * P:6 * P]]
        msk = [cst[:, 6 * P:7 * P], cst[:, 7 * P:8 * P]]
        identb = cstb[:, 0:P]
        onesb = cstb[:, P:2 * P]
        V = lambda key, i=0, n=1: vec[:, voff[key] + i: voff[key] + i + n]

        woutb = BIGA[:, :].bitcast(BF16) if 2 * T >= KC * DM else sb("woutb", [P, KC * DM], BF16)[:, :]
        qT = winb[:, 0:T]
        kT = winb[:, T:2 * T]
        VTt = winb[:, 2 * T:3 * T]
        Bwin, BA, BB = B("winb"), B("BIGA"), B("BIGB")

        def dma(out, in_, reads, writes, eng="sp", key=None):
            A.op(eng, lambda e: e.dma_start(out=out, in_=in_), reads=reads, writes=writes, dma=True, key=key)

        def act(out, in_, func, reads, writes, bias=None, scale=None):
            kw = {}
            if bias is not None:
                kw["bias"] = bias
            if scale is not None:
                kw["scale"] = scale
            A.op("act", lambda e: e.activation(out=out, in_=in_, func=func, **kw), reads=reads, writes=writes)

        def mm(out, lhsT, rhs, reads, writes, start=True, stop=True):
            A.op("pe", lambda e: e.matmul(out, lhsT=lhsT, rhs=rhs, start=start, stop=stop), reads=reads, writes=writes)

        def tt(out, in0, in1, op, reads, writes, eng="dve"):
            A.op(eng, lambda e: e.tensor_tensor(out=out, in0=in0, in1=in1, op=op), reads=reads, writes=writes)

        def ts(out, in0, s1, s2, op0, op1, reads, writes, eng="dve"):
            if s2 is None:
                A.op(eng, lambda e: e.tensor_scalar(out=out, in0=in0, scalar1=s1, scalar2=None, op0=op0), reads=reads, writes=writes)
            else:
                A.op(eng, lambda e: e.tensor_scalar(out=out, in0=in0, scalar1=s1, scalar2=s2, op0=op0, op1=op1), reads=reads, writes=writes)

        def stt(out, in0, s, in1, op0, op1, reads, writes, eng="dve"):
            A.op(eng, lambda e: e.scalar_tensor_tensor(out=out, in0=in0, scalar=s, in1=in1, op0=op0, op1=op1), reads=reads, writes=writes)

        def cp(out, in_, reads, writes, eng="dve"):
            if eng == "act":
                act(out, in_, AF.Copy, reads, writes)
            else:
                A.op(eng, lambda e: e.tensor_copy(out=out, in_=in_), reads=reads, writes=writes)

        def recip(out, in_, reads, writes):
            A.op("dve", lambda e: e.reciprocal(out=out, in_=in_), reads=reads, writes=writes)

        Bv, Bc = B("vec"), B("cst")
        dma(vec[:, :], vec_d[:, :], [], [Bv])
        dma(cst[:, :], cst_d[:, :], [], [Bc])
        cp(cstb[:, :], cst[:, 0:2 * P], [Bc], [B("cstb")])
        Bcb = B("cstb")
        Beps = B("epsc")
        A.op("pool", lambda e: e.memset(epsc[:, 0:1], 1e-6), writes=[Beps])
        A.op("pool", lambda e: e.memset(epsc[:, 1:2], 1e-5), writes=[Beps])
        A.op("pool", lambda e: e.memset(epsc[:, 2:3], 1.0), writes=[Beps])
        A.op("pool", lambda e: e.memset(epsc[:, 3:4], 0.0), writes=[Beps])
        EPS6, EPS5, ONE1 = epsc[:, 0:1], epsc[:, 1:2], epsc[:, 2:3]
        act(sc[:, :], V("cond", 0, 16), AF.Silu, [Bv], [B("sc")])

        rr = {"ev": 0, "pA": 0, "sq": 0}

        def nxt(k, n):
            i = rr[k]
            rr[k] = (i + 1) % n
            return i

        def load_weight_bf16(dst, dstB, src_rows, ncols, coloff=0):
            c0 = 0
            i = 0
            while c0 < ncols:
                w = min(1024, ncols - c0)
                dma(wst[:, 0:w], src_rows[:, c0:c0 + w], [], [B("wst")])
                eng = ("act", "dve", "pool")[i % 3]
                cp(dst[:, coloff + c0: coloff + c0 + w], wst[:, 0:w], [B("wst")], [dstB], eng=eng)
                c0 += w
                i += 1

        def rms_stats(x3, xB, nb, out_rstd, inv_n, epsap):
            for k in range(KC):
                si = nxt("sq", 2)
                act(sqb[si][:, 0:nb], x3(k), AF.Square, [xB], [B("sqb%d" % si)])
                mm(pS[:, 0:nb], onesb, sqb[si][:, 0:nb], [Bcb, B("sqb%d" % si)], [B("pS")], start=(k == 0), stop=(k == KC - 1))
            act(out_rstd[:, 0:nb], pS[:, 0:nb], AF.Sqrt, [B("pS"), Beps], [B("rstd")], bias=epsap, scale=inv_n)
            recip(out_rstd[:, 0:nb], out_rstd[:, 0:nb], [B("rstd")], [B("rstd")])

        for l in range(L):
            last = (l == L - 1)
            xsrc = xT if l == 0 else XS
            xsrcB = B("xT") if l == 0 else B("XS")
            Bmod = B("modt")
            for k in range(KC):
                for piece in range(3):
                    dma(wst[:, :], w_ada[l, k * P:(k + 1) * P, piece * 1024:(piece + 1) * 1024], [], [B("wst")])
                    for jj in range(8):
                        j = piece * 8 + jj
                        mm(pM[:, j * 2:j * 2 + 2], wst[:, jj * P:(jj + 1) * P], sc[:, k * 2:k * 2 + 2], [B("wst"), B("sc")], [B("pM")])
                if k == 0:
                    cp(modt[:, :], pM[:, 0:48], [B("pM")], [Bmod])
                else:
                    tt(modt[:, :], modt[:, :], pM[:, 0:48], OP.add, [B("pM"), Bmod], [Bmod])
            for j in range(24):
                ts(modt[:, 2 * j:2 * j + 2], modt[:, 2 * j:2 * j + 2], V(("bada", l), j), None, OP.add, None, [Bmod, Bv], [Bmod])
            for k in range(KC):
                ts(gwt[:, 2 * k:2 * k + 2], modt[:, 2 * (8 + k):2 * (8 + k) + 2], 1.0, None, OP.add, None, [Bmod], [B("gwt")])
                ts(gwt[:, 2 * k:2 * k + 2], gwt[:, 2 * k:2 * k + 2], V(("normw", l), k), None, OP.mult, None, [B("gwt"), Bv], [B("gwt")])
            SH = lambda k, c: modt[:, 2 * k + c:2 * k + c + 1]
            GW = lambda k, c: gwt[:, 2 * k + c:2 * k + c + 1]
            GATE = lambda k, c: modt[:, 2 * (16 + k) + c:2 * (16 + k) + c + 1]

            if STOP < 1:
                break
            for k in range(KC):
                load_weight_bf16(winb, Bwin, w_in[l, k * P:(k + 1) * P, :], DIN, coloff=k * DIN)

            xv = xb[:, :].rearrange("p (k n) -> p k n", k=KC)
            hv = hb[:, :].rearrange("p (k n) -> p k n", k=KC)
            Bx, Bh = B("xb"), B("hb")
            for (t0, nb, c) in blocks:
                dma(xv[:, :, 0:nb], xsrc.rearrange("(k p) t -> p k t", p=P)[:, :, t0:t0 + nb], [xsrcB], [Bx])
                rms_stats(lambda k: xv[:, k, 0:nb], Bx, nb, rstd, 1.0 / DM, EPS6)
                for k in range(KC):
                    i = nxt("ev", 4)
                    tt(ev[i][:, 0:nb], xv[:, k, 0:nb], rstd[:, 0:nb], OP.mult, [Bx, B("rstd")], [B("ev%d" % i)])
                    act(hv[:, k, 0:nb], ev[i][:, 0:nb], AF.Identity, [B("ev%d" % i), Bmod, B("gwt")], [Bh], bias=SH(k, c), scale=GW(k, c))

                def proj(ct, w=P):
                    pi = nxt("pA", 2)
                    for k in range(KC):
                        mm(pA[pi][0:w, 0:nb], winb[:, k * DIN + ct * P: k * DIN + ct * P + w], hv[:, k, 0:nb], [Bwin, Bh], [B("pA%d" % pi)], start=(k == 0), stop=(k == KC - 1))
                    return pA[pi], B("pA%d" % pi)

                skip_conv = last and c == 0
                if not skip_conv:
                    for ct in range(4):
                        pg, pgB = proj(ct + 4)
                        i = nxt("ev", 4)
                        act(ev[i][:, 0:nb], pg[:, 0:nb], AF.Sigmoid, [pgB], [B("ev%d" % i)])
                        pa_, paB = proj(ct)
                        j = nxt("ev", 4)
                        tt(ev[j][:, 0:nb], pa_[:, 0:nb], ev[i][:, 0:nb], OP.mult, [paB, B("ev%d" % i)], [B("ev%d" % j)])
                        dma(Ud[ct * P:(ct + 1) * P, t0:t0 + nb], ev[j][:, 0:nb], [B("ev%d" % j)], [B("Ud")], eng="pool", key="st_ev%d" % j)
                    for ct in range(4):
                        pg, pgB = proj(8 + ct)
                        i = nxt("ev", 4)
                        act(ev[i][:, 0:nb], pg[:, 0:nb], AF.Silu, [pgB], [B("ev%d" % i)])
                        dma(CGd[ct * P:(ct + 1) * P, t0:t0 + nb], ev[i][:, 0:nb], [B("ev%d" % i)], [B("CGd")], eng="pool", key="st_ev%d" % i)
                for ct in range(12):
                    pg, pgB = proj(12 + ct)
                    i = nxt("ev", 4)
                    cp(ev[i][:, 0:nb], pg[:, 0:nb], [pgB], [B("ev%d" % i)], eng=("dve" if ct % 2 else "act"))
                    dma(QKVd[ct * P:(ct + 1) * P, t0:t0 + nb], ev[i][:, 0:nb], [B("ev%d" % i)], [B("QKVd")], eng="pool", key="st_ev%d" % i)
                for ct in range(4):
                    pg, pgB = proj(24 + ct)
                    i = nxt("ev", 4)
                    act(ev[i][:, 0:nb], pg[:, 0:nb], AF.Silu, [pgB], [B("ev%d" % i)])
                    dma(ZGd[ct * P:(ct + 1) * P, t0:t0 + nb], ev[i][:, 0:nb], [B("ev%d" % i)], [B("ZGd")], eng="pool", key="st_ev%d" % i)
                for tt_i in range(nb // P):
                    ti = t0 // P + tt_i
                    for k in range(KC):
                        mm(pM[:, 64:80], hv[:, k, tt_i * P:(tt_i + 1) * P], winb[:, k * DIN + 3584:k * DIN + 3600], [Bh, Bwin], [B("pM")], start=(k == 0), stop=(k == KC - 1))
                    cp(ABT[:, ti * 16:(ti + 1) * 16], pM[:, 64:80], [B("pM")], [B("ABT")])

            if STOP < 2:
                break
            abv = ABT[:, :].rearrange("p (t r) -> p t r", r=16)
            S3 = {n: SC[n][:, :].rearrange("p (r t) -> p r t", r=8) for n in SC}
            BS = B("SC")
            act(eal[:, :], V(("alog", l), 0, 8), AF.Exp, [Bv], [B("eal")])
            ts(eal[:, :], eal[:, :], -1.0, None, OP.mult, None, [B("eal")], [B("eal")])
            for r in range(8):
                ts(tmpc[:, :], abv[:, :, r], V(("dtb", l), r), None, OP.add, None, [B("ABT"), Bv], [B("tmpc")])
                act(tmpc2[:, :], tmpc[:, :], AF.Abs, [B("tmpc")], [B("tmpc2")])
                act(tmpc2[:, :], tmpc2[:, :], AF.Exp, [B("tmpc2")], [B("tmpc2")], scale=-1.0)
                act(tmpc2[:, :], tmpc2[:, :], AF.Ln, [B("tmpc2"), Beps], [B("tmpc2")], bias=ONE1)
                stt(tmpc[:, :], tmpc[:, :], 0.0, tmpc2[:, :], OP.max, OP.add, [B("tmpc"), B("tmpc2")], [B("tmpc")])
                ts(S3["G"][:, r, :], tmpc[:, :], eal[:, r:r + 1], None, OP.mult, None, [B("tmpc"), B("eal")], [BS])
                act(S3["BE"][:, r, :], abv[:, :, 8 + r], AF.Sigmoid, [B("ABT")], [BS])
            for dr in range(2):
                sl = slice(dr * 4 * NT, (dr + 1) * 4 * NT)
                mm(pS[:, 0:4 * NT], tri[dr], SC["G"][:, sl], [Bc, BS], [B("pS")])
                cp(SC["GC"][:, sl], pS[:, 0:4 * NT], [B("pS")], [BS])
                mm(pS[:, 0:4 * NT], ones, SC["G"][:, sl], [Bc, BS], [B("pS")])
                cp(SC["GT"][:, sl], pS[:, 0:4 * NT], [B("pS")], [BS])
            act(SC["EG"][:, :], SC["GC"][:, :], AF.Exp, [BS], [BS])
            act(SC["GL"][:, :], SC["GT"][:, :], AF.Exp, [BS], [BS])
            tt(SC["EKT"][:, :], SC["GT"][:, :], SC["GC"][:, :], OP.subtract, [BS], [BS])
            act(SC["EKT"][:, :], SC["EKT"][:, :], AF.Exp, [BS], [BS])
            ts(SC["NBE"][:, :], SC["BE"][:, :], -1.0, None, OP.mult, None, [BS], [BS])
            tt(SC["BKG"][:, :], SC["BE"][:, :], SC["EG"][:, :], OP.mult, [BS], [BS])
            SCL = lambda n, dr, hd, ti: SC[n][:, (dr * 4 + hd) * NT + ti:(dr * 4 + hd) * NT + ti + 1]

            if STOP < 3:
                break
            def dwconv(src, srcBs, dst, dstBs, wcol, ntap, first_bias, mode, do_ctx=True):
                half = (ntap - 1) // 2
                segs = []
                if do_ctx:
                    segs.append(("ctx", "dve"))
                segs.append(("lat", "dve"))
                for seg, eng in segs:
                    sB, dB = srcBs[seg], dstBs[seg]
                    lo, hi = (0, TC) if seg == "ctx" else (TC, T)
                    if first_bias is None:
                        ts(dst[:, lo:hi], src[:, lo:hi], wcol(half), None, OP.mult, None, [sB, Bv], [dB], eng=eng)
                    else:
                        ts(dst[:, lo:hi], src[:, lo:hi], wcol(half), first_bias, OP.mult, OP.add, [sB, Bv], [dB], eng=eng)
                    m = "seq" if seg == "ctx" else mode
                    for k in range(ntap):
                        s = k - half
                        if s == 0:
                            continue
                        if m == "seq":
                            n = hi - lo
                            a0, a1 = max(0, -s), n - max(0, s)
                            if a1 <= a0:
                                continue
                            o_ap = dst[:, lo + a0:lo + a1]
                            i_ap = src[:, lo + a0 + s:lo + a1 + s]
                        elif m == "h":
                            dv_ = dst[:, lo:hi].rearrange("p (r c) -> p r c", c=64)
                            sv_ = src[:, lo:hi].rearrange("p (r c) -> p r c", c=64)
                            a0, a1 = max(0, -s), 64 - max(0, s)
                            o_ap = dv_[:, :, a0:a1]
                            i_ap = sv_[:, :, a0 + s:a1 + s]
                        else:
                            r0, r1 = max(0, -s), R - max(0, s)
                            if r1 <= r0:
                                continue
                            o_ap = dst[:, lo + 64 * r0:lo + 64 * r1]
                            i_ap = src[:, lo + 64 * (r0 + s):lo + 64 * (r1 + s)]
                        stt(o_ap, i_ap, wcol(k), o_ap, OP.mult, OP.add, [sB, dB, Bv], [dB], eng=eng)

            BAs = {"ctx": BA, "lat": BA}
            BBs = {"ctx": BB, "lat": BB}
            allA = [BA]
            allB = [BB]
            for ct in range(4):
                dma(BIGA[:, :], Ud[ct * P:(ct + 1) * P, :], [B("Ud")], allA)
                dwconv(BIGA, BAs, BIGB, BBs, lambda k, ct=ct: V(("convw", l), ct * 31 + k), 31,
                       V(("convb", l), ct), "h" if ct < 2 else "v", do_ctx=not last)
                dma(CVd[ct * P:(ct + 1) * P, :], BIGB[:, :], allB, [B("CVd")], eng="pool", key="st_BIGB")

            if STOP < 4:
                break
            psum_fence()
            Ov = BIGB[:, :].rearrange("p (t d) -> p t d", d=P)
            fwd_order = list(range(NT))
            bwd_order = [1, 0] + list(range(NT - 1, 1, -1))
            for hd in range(4):
                for m_i, dstT in ((0, qT), (1, kT), (2, VTt)):
                    if m_i >= int(os.environ.get('KSUB', '3')):
                        continue
                    row0 = m_i * 512 + hd * P
                    dma(BIGA[:, :], QKVd[row0:row0 + P, :], [B("QKVd")], allA)
                    dwconv(BIGA, BAs, BIGB, BBs, lambda k, m_i=m_i: V(("scw", l), (m_i * 4 + hd) * 5 + k), 5, None, "seq")
                    act(BIGB[:, :], BIGB[:, :], AF.Silu, [BB], allB)
                    if m_i < 2:
                        for t0 in range(0, T, NB):
                            si = nxt("sq", 2)
                            act(sqb[si][:, :], BIGB[:, t0:t0 + NB], AF.Square, [BB], [B("sqb%d" % si)])
                            mm(pS[:, 0:NB], onesb, sqb[si][:, :], [Bcb, B("sqb%d" % si)], [B("pS")])
                            act(rstd[:, :], pS[:, 0:NB], AF.Sqrt, [B("pS"), Beps], [B("rstd")], bias=EPS6, scale=1.0)
                            recip(rstd[:, :], rstd[:, :], [B("rstd")], [B("rstd")])
                            if m_i == 0:
                                stt(dstT[:, t0:t0 + NB], BIGB[:, t0:t0 + NB], float(P) ** -0.5, rstd[:, :], OP.mult, OP.mult, [BB, B("rstd")], [Bwin])
                            else:
                                tt(dstT[:, t0:t0 + NB], BIGB[:, t0:t0 + NB], rstd[:, :], OP.mult, [BB, B("rstd")], [Bwin])
                    else:
                        for ti in range(NT):
                            ps_, psB = pM[:, 384:512], B("pM")
                            si = nxt("sq", 2)
                            cp(sqb[si][:, 0:P], BIGB[:, ti * P:(ti + 1) * P], [BB], [B("sqb%d" % si)], eng="act")
                            mm(ps_, sqb[si][:, 0:P], identb, [B("sqb%d" % si), Bcb], [psB])
                            cp(VTt[:, ti * P:(ti + 1) * P], ps_, [psB], [Bwin], eng="dve")

                if STOP < 5:
                    continue
                def big_fence():
                    ws = [BA, B("fz")] + [B("%s_%d" % (n, sl)) for sl in (2, 3) for n in DTF + DTB]
                    A.op("pool", lambda e: e.memset(fz[:, :], 0.0), writes=ws)

                big_fence()
                for dr in range(2):
                    A.op("pool", lambda e, dr=dr: e.memset(ST[("Sf", dr)], 0.0), writes=[B("Sf%d" % dr)])
                    A.op("pool", lambda e, dr=dr: e.memset(ST[("Sb", dr)], 0.0), writes=[B("Sb%d" % dr)])
                seen = set()

                def setup_ops(sl, dr, ti):
                    d = lambda n: DT[(n, sl)]
                    bb = lambda n: B("%s_%d" % (n, sl))
                    tsl = slice(ti * P, (ti + 1) * P)
                    pKK, pKKB = pd(0 + 6 * sl)
                    pQK, pQKB = pd(1 + 6 * sl)
                    pZ, pZB = pd(2 + 6 * sl)
                    p3, p3B = pd(3 + 6 * sl)
                    p4, p4B = pd(4 + 6 * sl)
                    p5, p5B = pd(5 + 6 * sl)
                    gc = SCL("GC", dr, hd, ti)
                    ops = []
                    ops.append(lambda: (mm(pKK, kT[:, tsl], kT[:, tsl], [Bwin], [pKKB]),
                                        ts(d("Gd"), ident, gc, None, OP.mult, None, [Bc, BS], [bb("Gd")])))
                    ops.append(lambda: (mm(pZ, ones, d("Gd"), [Bc, bb("Gd")], [pZB], start=True, stop=False),
                                        mm(pZ, ident, pen[dr], [Bc], [pZB], start=False, stop=True)))
                    ops.append(lambda: (ts(d("D"), pZ, gc, None, OP.subtract, None, [pZB, BS], [bb("D")]),
                                        mm(pQK, qT[:, tsl], kT[:, tsl], [Bwin], [pQKB])))
                    ops.append(lambda: act(d("D"), d("D"), AF.Exp, [bb("D")], [bb("D")], scale=-1.0))
                    ops.append(lambda: tt(d("DMs"), d("D"), msk[dr], OP.mult, [bb("D"), Bc], [bb("DMs")], eng="pool"))
                    ops.append(lambda: (stt(d("N0"), pKK, SCL("NBE", dr, hd, ti), d("DMs"), OP.mult, OP.mult, [pKKB, BS, bb("DMs")], [bb("N0")]),
                                        tt(d("attn"), pQK, d("D"), OP.mult, [pQKB, bb("D")], [bb("attn")])))
                    ops.append(lambda: (mm(p4, d("N0"), ident, [bb("N0"), Bc], [p4B]),
                                        mm(p3, d("attn"), identb, [bb("attn"), Bcb], [p3B]),
                                        mm(p5, kT[:, tsl], identb, [Bwin, Bcb], [p5B])))
                    ops.append(lambda: (cp(d("U0"), p4, [p4B], [bb("U0")], eng="act"),
                                        cp(d("attnT"), p3, [p3B], [bb("attnT")], eng="act"),
                                        ts(d("kbg"), p5, SCL("BKG", dr, hd, ti), None, OP.mult, None, [p5B, BS], [bb("kbg")])))
                    ops.append(lambda: (tt(d("Y"), d("U0"), ident, OP.add, [bb("U0"), Bc], [bb("Y")], eng="pool"),
                                        ts(d("kt"), p5, SCL("EKT", dr, hd, ti), None, OP.mult, None, [p5B, BS], [bb("kt")]),
                                        act(d("vb"), VTt[:, tsl], AF.Identity, [Bwin, BS], [bb("vb")], scale=SCL("BE", dr, hd, ti))))
                    return ops

                for pair in range(NT // 2):
                    chains = []
                    for j in range(2):
                        step = 2 * pair + j
                        chains.append((2 * j + 0, 0, fwd_order[step]))
                        chains.append((2 * j + 1, 1, bwd_order[step]))
                    allops = [setup_ops(sl, dr, ti) for sl, dr, ti in chains]
                    for i in range(len(allops[0])):
                        for ops in allops:
                            ops[i]()
                    cur = {sl: ("N0", "U0") for sl, _, _ in chains}
                    for lvl in range(1, 7):
                        info = []
                        for sl, dr, ti in chains:
                            d = lambda n, sl=sl: DT[(n, sl)]
                            bb = lambda n, sl=sl: B("%s_%d" % (n, sl))
                            nN, nU = cur[sl]
                            oN, oU = ("N1", "U1") if nN == "N0" else ("N0", "U0")
                            pa_, paB = pd(3 + 6 * sl)
                            pb_, pbB = pd(4 + 6 * sl)
                            mm(pb_, d(nU), d(nN), [bb(nU), bb(nN)], [pbB])
                            if lvl < 6:
                                mm(pa_, d(nN), d(nU), [bb(nU), bb(nN)], [paB])
                            info.append((sl, d, bb, oN, oU, pa_, paB, pb_, pbB))
                            cur[sl] = (oN, oU)
                        for sl, d, bb, oN, oU, pa_, paB, pb_, pbB in info:
                            cp(d(oN), pb_, [pbB], [bb(oN)], eng=("act" if sl % 2 else "dve"))
                            if lvl < 6:
                                cp(d(oU), pa_, [paB], [bb(oU)], eng=("dve" if sl % 2 else "act"))
                        for sl, d, bb, oN, oU, pa_, paB, pb_, pbB in info:
                            pc_, pcB = pd(5 + 6 * sl)
                            mm(pc_, d(oN), d("Y"), [bb(oN), bb("Y")], [pcB])
                        for sl, d, bb, oN, oU, pa_, paB, pb_, pbB in info:
                            pc_, pcB = pd(5 + 6 * sl)
                            tt(d("Y"), d("Y"), pc_, OP.add, [pcB, bb("Y")], [bb("Y")])
                    for sl, dr, ti in chains:
                        d = lambda n, sl=sl: DT[(n, sl)]
                        bb = lambda n, sl=sl: B("%s_%d" % (n, sl))
                        cp(d("TTb"), d("Y"), [bb("Y")], [bb("TTb")], eng="act")
                    for sl, dr, ti in chains:
                        d = lambda n, sl=sl: DT[(n, sl)]
                        bb = lambda n, sl=sl: B("%s_%d" % (n, sl))
                        pa_, paB = pd(3 + 6 * sl)
                        pb_, pbB = pd(4 + 6 * sl)
                        mm(pa_, d("TTb"), d("vb"), [bb("TTb"), bb("vb")], [paB])
                        mm(pb_, d("kbg"), d("TTb"), [bb("kbg"), bb("TTb")], [pbB])
                    for sl, dr, ti in chains:
                        d = lambda n, sl=sl: DT[(n, sl)]
                        bb = lambda n, sl=sl: B("%s_%d" % (n, sl))
                        pa_, paB = pd(3 + 6 * sl)
                        pb_, pbB = pd(4 + 6 * sl)
                        cp(d("u"), pa_, [paB], [bb("u")], eng="act")
                        cp(d("wT"), pb_, [pbB], [bb("wT")], eng="dve")
                    for j in range(2):
                        cj = [c for c in chains if c[0] // 2 == j]
                        for sl, dr, ti in cj:
                            d = lambda n, sl=sl: DT[(n, sl)]
                            bb = lambda n, sl=sl: B("%s_%d" % (n, sl))
                            Sf, Sbb = ST[("Sf", dr)], ST[("Sb", dr)]
                            SfB, SbB = B("Sf%d" % dr), B("Sb%d" % dr)
                            tsl = slice(ti * P, (ti + 1) * P)
                            pc_, pcB = pd(5 + 6 * sl)
                            pe_, peB = pd(0 + 6 * sl)
                            pf_, pfB = pd(1 + 6 * sl)
                            pg_, pgB = pd(2 + 6 * sl)
                            mm(pc_, d("wT"), Sbb, [bb("wT"), SbB], [pcB])
                            mm(pe_, qT[:, tsl], Sbb, [Bwin, SbB], [peB])
                            tt(d("vn"), d("u"), pc_, OP.subtract, [bb("u"), pcB], [bb("vn")])
                            mm(pg_, d("kt"), d("vn"), [bb("kt"), bb("vn")], [pgB])
                            mm(pf_, d("attnT"), d("vn"), [bb("attnT"), bb("vn")], [pfB])
                            stt(Sf, Sf, SCL("GL", dr, hd, ti), pg_, OP.mult, OP.add, [SfB, BS, pgB], [SfB])
                            cp(Sbb, Sf, [SfB], [SbB], eng="act")
                            ts(d("t"), pe_, SCL("EG", dr, hd, ti), None, OP.mult, None, [peB, BS], [bb("t")])
                            if ti in seen:
                                tt(Ov[:, ti, :], Ov[:, ti, :], d("t"), OP.add, [bb("t"), BB], [BB])
                                tt(Ov[:, ti, :], Ov[:, ti, :], pf_, OP.add, [pfB, BB], [BB])
                            else:
                                tt(Ov[:, ti, :], d("t"), pf_, OP.add, [bb("t"), pfB], [BB])
                                seen.add(ti)
                big_fence()

                if int(os.environ.get('KSC', '3')) < 3 or int(os.environ.get('KOUT', '1')) < 1:
                    continue
                do_out_ctx = not last
                act(BIGA[:, :], BIGB[:, :], AF.Square, [BB], allA)
                A.op("dve", lambda e: e.tensor_reduce(out=orstd[:, :], in_=BIGA[:, :].rearrange("p (t d) -> p t d", d=P), axis=AX.X, op=OP.add), reads=[BA], writes=[B("orstd")])
                act(orstd[:, :], orstd[:, :], AF.Sqrt, [B("orstd"), Beps], [B("orstd")], bias=EPS6, scale=1.0 / P)
                recip(orstd[:, :], orstd[:, :], [B("orstd")], [B("orstd")])
                dma(BIGA[:, :], ZGd[hd * P:(hd + 1) * P, :], [B("ZGd")], allA)
                for ti in range(0 if do_out_ctx else 2, NT):
                    stt(ont[:, :], Ov[:, ti, :], orstd[:, ti:ti + 1], V(("dnw", l), 0, P), OP.mult, OP.mult, [BB, B("orstd"), Bv], [B("ont")])
                    ps_, psB = pM[:, 384:512], B("pM")
                    mm(ps_, ont[:, :], ident, [B("ont"), Bc], [psB])
                    tt(ydt[:, :], ps_, BIGA[:, ti * P:(ti + 1) * P], OP.mult, [psB, BA], [B("ydt")])
                    dma(YDd[hd * P:(hd + 1) * P, ti * P:(ti + 1) * P], ydt[:, :], [B("ydt")], [B("YDd")], eng="pool", key="st_ydt")

            if STOP < 6:
                break
            psum_fence()
            for k in range(KC):
                load_weight_bf16(woutb, BA, w_out[l, k * P:(k + 1) * P, :], DM, coloff=k * DM)
            cvv = cvb[:, :].rearrange("p (k n) -> p k n", k=4)
            cgv = cgb[:, :].rearrange("p (k n) -> p k n", k=4)
            for (t0, nb, c) in blocks:
                if last and c == 0:
                    continue
                dma(cvv[:, :, 0:nb], CVd.rearrange("(k p) t -> p k t", p=P)[:, :, t0:t0 + nb], [B("CVd")], [B("cvb")])
                dma(cgv[:, :, 0:nb], CGd.rearrange("(k p) t -> p k t", p=P)[:, :, t0:t0 + nb], [B("CGd")], [B("cgb")])
                dma(hv[:, 4:8, 0:nb], YDd.rearrange("(k p) t -> p k t", p=P)[:, :, t0:t0 + nb], [B("YDd")], [Bh])
                dma(xv[:, :, 0:nb], xsrc.rearrange("(k p) t -> p k t", p=P)[:, :, t0:t0 + nb], [xsrcB], [Bx])
                for ct in range(4):
                    mm(pS[:, 0:nb], ones, cvv[:, ct, 0:nb], [Bc, B("cvb")], [B("pS")], start=(ct == 0), stop=(ct == 3))
                act(mean[:, 0:nb], pS[:, 0:nb], AF.Copy, [B("pS")], [B("mean")], scale=1.0 / 512)
                for ct in range(4):
                    i = nxt("ev", 4)
                    act(ev[i][:, 0:nb], cvv[:, ct, 0:nb], AF.Square, [B("cvb")], [B("ev%d" % i)])
                    mm(pM[:, 0:nb], ones, ev[i][:, 0:nb], [Bc, B("ev%d" % i)], [B("pM")], start=(ct == 0), stop=(ct == 3))
                i = nxt("ev", 4)
                tt(ev[i][:, 0:nb], mean[:, 0:nb], mean[:, 0:nb], OP.mult, [B("mean")], [B("ev%d" % i)])
                stt(rstd[:, 0:nb], pM[:, 0:nb], 1.0 / 512, ev[i][:, 0:nb], OP.mult, OP.subtract, [B("pM"), B("ev%d" % i)], [B("rstd")])
                act(rstd[:, 0:nb], rstd[:, 0:nb], AF.Sqrt, [B("rstd"), Beps], [B("rstd")], bias=EPS5, scale=1.0)
                recip(rstd[:, 0:nb], rstd[:, 0:nb], [B("rstd")], [B("rstd")])
                for ct in range(4):
                    i = nxt("ev", 4)
                    tt(ev[i][:, 0:nb], cvv[:, ct, 0:nb], mean[:, 0:nb], OP.subtract, [B("cvb"), B("mean")], [B("ev%d" % i)])
                    tt(ev[i][:, 0:nb], ev[i][:, 0:nb], rstd[:, 0:nb], OP.mult, [B("ev%d" % i), B("rstd")], [B("ev%d" % i)])
                    act(ev[i][:, 0:nb], ev[i][:, 0:nb], AF.Silu, [B("ev%d" % i), Bv], [B("ev%d" % i)], bias=V(("lnb", l), ct), scale=V(("lnw", l), ct))
                    tt(hv[:, ct, 0:nb], ev[i][:, 0:nb], cgv[:, ct, 0:nb], OP.mult, [B("ev%d" % i), B("cgb")], [Bh])
                for fc in range(KC):
                    pi = nxt("pA", 2)
                    for k in range(KC):
                        mm(pA[pi][:, 0:nb], woutb[:, k * DM + fc * P:k * DM + (fc + 1) * P], hv[:, k, 0:nb], [BA, Bh], [B("pA%d" % pi)], start=(k == 0), stop=(k == KC - 1))
                    stt(xv[:, fc, 0:nb], pA[pi][:, 0:nb], GATE(fc, c), xv[:, fc, 0:nb], OP.mult, OP.add, [B("pA%d" % pi), Bmod, Bx], [Bx])
                if not last:
                    dma(XS.rearrange("(k p) t -> p k t", p=P)[:, :, t0:t0 + nb], xv[:, :, 0:nb], [Bx], [B("XS")], eng="pool", key="st_xb")
                else:
                    rms_stats(lambda k: xv[:, k, 0:nb], Bx, nb, rstd, 1.0 / DM, EPS6)
                    for k in range(KC):
                        stt(xv[:, k, 0:nb], xv[:, k, 0:nb], V("fnw", k), rstd[:, 0:nb], OP.mult, OP.mult, [Bx, Bv, B("rstd")], [Bx])
                    dma(outT.rearrange("(k p) t -> p k t", p=P)[:, :, t0 - TC:t0 - TC + nb], xv[:, :, 0:nb], [Bx], [B("outT")], eng="pool", key="st_xb")

        A.final_waits = ["st_xb"]
        A.emit(nc)
    return nc


def make_consts():
    i = np.arange(P)
    c = np.zeros((P, 8, P), np.float32)
    c[:, 0] = np.eye(P)
    c[:, 1] = 1.0
    c[:, 2] = (i[:, None] <= i[None, :])
    c[:, 3] = (i[:, None] >= i[None, :])
    c[:, 4] = np.where(i[:, None] >= i[None, :], 0.0, BIG)
    c[:, 5] = np.where(i[:, None] <= i[None, :], 0.0, BIG)
    c[:, 6] = (i[:, None] > i[None, :])
    c[:, 7] = (i[:, None] < i[None, :])
    return c.reshape(P, 8 * P)


def make_vec(L, b, c, c_ctx, norm_w, b_ada, conv_w, conv_b, conv_ln_w, conv_ln_b, short_conv_w, a_log,
             dt_bias, dn_norm_w, final_norm_w):
    voff, NV = vec_layout(L)
    v = np.zeros((P, NV), np.float32)
    pk = lambda a, n: np.asarray(a, np.float32).reshape(n, P).T
    for l in range(L):
        v[:, voff[("normw", l)]:voff[("normw", l)] + 8] = pk(norm_w[l], 8)
        v[:, voff[("bada", l)]:voff[("bada", l)] + 24] = pk(b_ada[l], 24)
        cw = np.asarray(conv_w[l], np.float32)
        v[:, voff[("convw", l)]:voff[("convw", l)] + 124] = cw.reshape(31, 4, P).transpose(2, 1, 0).reshape(P, 124)
        v[:, voff[("convb", l)]:voff[("convb", l)] + 4] = pk(conv_b[l], 4)
        v[:, voff[("lnw", l)]:voff[("lnw", l)] + 4] = pk(conv_ln_w[l], 4)
        v[:, voff[("lnb", l)]:voff[("lnb", l)] + 4] = pk(conv_ln_b[l], 4)
        sw = np.asarray(short_conv_w[l], np.float32)
        v[:, voff[("scw", l)]:voff[("scw", l)] + 60] = sw.reshape(5, 12, P).transpose(2, 1, 0).reshape(P, 60)
        v[:, voff[("alog", l)]:voff[("alog", l)] + 8] = np.asarray(a_log[l], np.float32).reshape(1, 8)
        v[:, voff[("dtb", l)]:voff[("dtb", l)] + 8] = np.asarray(dt_bias[l], np.float32).reshape(1, 8)
        v[:, voff[("dnw", l)]:voff[("dnw", l)] + P] = np.asarray(dn_norm_w[l], np.float32).reshape(1, P)
    v[:, voff["fnw"]:voff["fnw"] + 8] = pk(final_norm_w, 8)
    cond = np.stack([pk(c_ctx, 8), pk(c[b], 8)], axis=-1).reshape(P, 16)
    v[:, voff["cond"]:voff["cond"] + 16] = cond
    return v


_prog_cache = {}


def kernel(x, c, ctx, c_ctx, norm_w, w_ada, b_ada, w_in, conv_w, conv_b, conv_ln_w, conv_ln_b,
           short_conv_w, a_log, dt_bias, dn_norm_w, w_out, final_norm_w):
    x = np.asarray(x, np.float32)
    ctx = np.asarray(ctx, np.float32)
    Bsz, SEQ, _ = x.shape
    L = np.asarray(norm_w).shape[0]
    R = SEQ // 64
    key = (R, L)
    if key not in _prog_cache:
        _prog_cache[key] = build_program(R, L)
    nc = _prog_cache[key]
    cstv = make_consts()
    w_ada_f = np.ascontiguousarray(np.asarray(w_ada, np.float32))
    w_in_f = np.ascontiguousarray(np.asarray(w_in, np.float32))
    w_out_f = np.ascontiguousarray(np.asarray(w_out, np.float32))
    in_maps = []
    for b in range(Bsz):
        xTb = np.ascontiguousarray(np.concatenate([ctx[b], x[b]], axis=0).T)
        vecb = make_vec(L, b, np.asarray(c, np.float32), np.asarray(c_ctx, np.float32), norm_w, b_ada, conv_w, conv_b,
                        conv_ln_w, conv_ln_b, short_conv_w, a_log, dt_bias, dn_norm_w, final_norm_w)
        in_maps.append({"xT": xTb, "vec_in": vecb, "cst_in": cstv, "w_ada": w_ada_f, "w_in": w_in_f, "w_out": w_out_f})
    res = run_bass_kernel_spmd(nc, in_maps, core_ids=list(range(Bsz)))
    out = np.stack([np.ascontiguousarray(r["outT"].T) for r in res.results], axis=0)
    return out.astype(np.float32)
```
